# Optimizing a Trainium2 kernel written in Bass

```python
import math
import jax, jax.numpy as jnp
from jax import lax
import numpy as np

D_MODEL = 2048
BATCH = 4
SEQ = 8192
DEPTH = 2
DEC_BATCH = 8
DEC_SEQ = 64
PAST_LEN = 2048

CHUNK = 64
N_META = 16
D_CONV_BR = 1024
CONV_A_W = 3
N_HEADS = 8
HEAD_DIM = 128
D_DELTA = N_HEADS * HEAD_DIM
D_QKV = 3 * D_DELTA
CONV_QKV_W = 4
D_MIX = D_CONV_BR + D_DELTA
PROJ_SIZES = (D_CONV_BR, D_CONV_BR, D_CONV_BR, D_CONV_BR, D_QKV, D_DELTA, N_HEADS, N_HEADS)
D_PROJ = sum(PROJ_SIZES)
PROJ_CUTS = [int(c) for c in np.cumsum(PROJ_SIZES)[:-1]]
EPS = 1e-6

kernel_name = "hymba_conv_gdn_stream_step"


def _rmsnorm(x, w):
    xf = x.astype(jnp.float32)
    y = xf * lax.rsqrt(jnp.mean(xf * xf, axis=-1, keepdims=True) + EPS) * w.astype(jnp.float32)
    return y.astype(x.dtype)


def _l2norm(x):
    return x * lax.rsqrt(jnp.sum(x * x, axis=-1, keepdims=True) + EPS)


def _causal_conv(u_full, w):
    width = w.shape[0]
    l = u_full.shape[1] - width + 1
    out = u_full[:, 0:l] * w[0]
    for j in range(1, width):
        out = out + u_full[:, j:j + l] * w[j]
    return out


def _gdn_chunked(q, k, v, g, beta, s0, chunk):
    b, l, h, _ = q.shape
    n = l // chunk

    def blk(t):
        t = t.reshape((b, n, chunk, h) + t.shape[3:])
        return jnp.moveaxis(t, (1, 3), (0, 2))

    qc, kc, vc, bc = blk(q), blk(k), blk(v), blk(beta)
    gc = jnp.cumsum(blk(g), axis=-1)
    idx = jnp.arange(chunk)
    causal = idx[:, None] >= idx[None, :]
    strict = idx[:, None] > idx[None, :]
    decay = jnp.exp(jnp.where(causal, gc[..., :, None] - gc[..., None, :], -jnp.inf))
    kk = jnp.einsum('nbhid,nbhjd->nbhij', kc * bc[..., None], kc) * decay
    eye = jnp.eye(chunk, dtype=kk.dtype)
    a_mat = jnp.where(strict, kk, 0.0) + eye
    t_mat = lax.linalg.triangular_solve(a_mat, jnp.broadcast_to(eye, a_mat.shape),
                                        left_side=True, lower=True, unit_diagonal=True)
    u = jnp.einsum('nbhij,nbhjd->nbhid', t_mat, vc * bc[..., None])
    w = jnp.einsum('nbhij,nbhjd->nbhid', t_mat, kc * (bc * jnp.exp(gc))[..., None])
    qk = jnp.einsum('nbhid,nbhjd->nbhij', qc, kc) * decay

    def step(s, inp):
        q_i, k_i, u_i, w_i, g_i, qk_i = inp
        v_new = u_i - jnp.einsum('bhcd,bhde->bhce', w_i, s)
        o = (jnp.einsum('bhcd,bhde->bhce', q_i * jnp.exp(g_i)[..., None], s)
             + jnp.einsum('bhij,bhje->bhie', qk_i, v_new))
        g_last = g_i[..., -1]
        s = (s * jnp.exp(g_last)[..., None, None]
             + jnp.einsum('bhcd,bhce->bhde', k_i * jnp.exp(g_last[..., None] - g_i)[..., None], v_new))
        return s, o

    s_fin, o = lax.scan(step, s0, (qc, kc, u, w, gc, qk))
    o = jnp.moveaxis(o, (0, 2), (1, 3)).reshape(b, l, h, v.shape[-1])
    return s_fin, o


def _layer(h, conv_a_prev, conv_qkv_prev, s_prev, n_meta,
           norm_w, w_in, conv_a_w, conv_qkv_w, a_log, dt_bias, o_norm_w, w_out):
    dt = h.dtype
    f32 = jnp.float32
    b, l, _ = h.shape
    xn = _rmsnorm(h, norm_w)
    proj = jnp.einsum('bld,dp->blp', xn, w_in)
    a_b, a_c, a_x, a_z, qkv, z_b, b_logit, a_logit = jnp.split(proj, PROJ_CUTS, axis=-1)

    u_full = jnp.concatenate([conv_a_prev.astype(dt), a_c * a_x], axis=1)
    y_a = a_b * _causal_conv(u_full, conv_a_w) * jax.nn.silu(a_z)
    new_conv_a = u_full[:, -(CONV_A_W - 1):]

    qkv_full = jnp.concatenate([conv_qkv_prev.astype(dt), qkv], axis=1)
    new_conv_qkv = qkv_full[:, -(CONV_QKV_W - 1):]
    qkv_c = jax.nn.silu(_causal_conv(qkv_full, conv_qkv_w)).astype(f32)
    q, k, v = jnp.split(qkv_c, 3, axis=-1)
    q = _l2norm(q.reshape(b, l, N_HEADS, HEAD_DIM)) * (HEAD_DIM ** -0.5)
    k = _l2norm(k.reshape(b, l, N_HEADS, HEAD_DIM))
    v = v.reshape(b, l, N_HEADS, HEAD_DIM)
    beta = jax.nn.sigmoid(b_logit.astype(f32))
    g = -jnp.exp(a_log.astype(f32)) * jax.nn.softplus(a_logit.astype(f32) + dt_bias.astype(f32))
    s = s_prev.astype(f32)
    if n_meta > 0:
        s, o_meta = _gdn_chunked(q[:, :n_meta], k[:, :n_meta], v[:, :n_meta],
                                 g[:, :n_meta], beta[:, :n_meta], s, n_meta)
        s, o_rest = _gdn_chunked(q[:, n_meta:], k[:, n_meta:], v[:, n_meta:],
                                 g[:, n_meta:], beta[:, n_meta:], s, CHUNK)
        o = jnp.concatenate([o_meta, o_rest], axis=1)
    else:
        s, o = _gdn_chunked(q, k, v, g, beta, s, min(l, CHUNK))
    o = _rmsnorm(o, o_norm_w) * jax.nn.silu(z_b.astype(f32).reshape(b, l, N_HEADS, HEAD_DIM))
    y_b = o.reshape(b, l, D_DELTA).astype(dt)

    y = jnp.einsum('blm,md->bld', jnp.concatenate([y_a, y_b], axis=-1), w_out)
    return h + y, new_conv_a, new_conv_qkv, s.astype(dt)


def setup_inputs(seed: int = 0) -> dict:
    key = jax.random.key(seed)
    ks = jax.random.split(key, 16)
    f32 = jnp.float32
    nrm = jax.random.normal
    x_prompt = nrm(ks[0], (BATCH, SEQ, D_MODEL), f32)
    x_sample = nrm(ks[1], (DEC_BATCH, DEC_SEQ, D_MODEL), f32)
    state_conv_a = nrm(ks[2], (DEPTH, DEC_BATCH, CONV_A_W - 1, D_CONV_BR), f32)
    state_conv_qkv = nrm(ks[3], (DEPTH, DEC_BATCH, CONV_QKV_W - 1, D_QKV), f32)
    state_delta = 0.1 * nrm(ks[4], (DEPTH, DEC_BATCH, N_HEADS, HEAD_DIM, HEAD_DIM), f32)
    meta_tokens = nrm(ks[5], (N_META, D_MODEL), f32)
    norm_w = 1.0 + 0.02 * nrm(ks[6], (DEPTH, D_MODEL), f32)
    w_in = nrm(ks[7], (DEPTH, D_MODEL, D_PROJ), f32) * (D_MODEL ** -0.5)
    conv_a_w = nrm(ks[8], (DEPTH, CONV_A_W, D_CONV_BR), f32) * (CONV_A_W ** -0.5)
    conv_qkv_w = nrm(ks[9], (DEPTH, CONV_QKV_W, D_QKV), f32) * (CONV_QKV_W ** -0.5)
    a_log = jnp.log(jax.random.uniform(ks[10], (DEPTH, N_HEADS), f32, 1.0, 16.0))
    dt0 = jnp.exp(jax.random.uniform(ks[11], (DEPTH, N_HEADS), f32, math.log(1e-3), math.log(1e-1)))
    dt_bias = dt0 + jnp.log(-jnp.expm1(-dt0))
    o_norm_w = 1.0 + 0.02 * nrm(ks[12], (DEPTH, HEAD_DIM), f32)
    w_out = nrm(ks[13], (DEPTH, D_MIX, D_MODEL), f32) * (D_MIX ** -0.5)
    final_norm_w = 1.0 + 0.02 * nrm(ks[14], (D_MODEL,), f32)
    return {"x_prompt": x_prompt, "x_sample": x_sample,
            "state_conv_a": state_conv_a, "state_conv_qkv": state_conv_qkv, "state_delta": state_delta,
            "meta_tokens": meta_tokens, "norm_w": norm_w, "w_in": w_in, "conv_a_w": conv_a_w,
            "conv_qkv_w": conv_qkv_w, "a_log": a_log, "dt_bias": dt_bias, "o_norm_w": o_norm_w,
            "w_out": w_out, "final_norm_w": final_norm_w}


def reference(x_prompt, x_sample, state_conv_a, state_conv_qkv, state_delta,
              meta_tokens, norm_w, w_in, conv_a_w, conv_qkv_w, a_log, dt_bias, o_norm_w,
              w_out, final_norm_w):
    dt = x_prompt.dtype
    b = x_prompt.shape[0]
    meta = jnp.broadcast_to(meta_tokens.astype(dt)[None], (b, N_META, D_MODEL))
    hp = jnp.concatenate([meta, x_prompt], axis=1)
    hs = x_sample
    zero_a = jnp.zeros((b, CONV_A_W - 1, D_CONV_BR), dt)
    zero_qkv = jnp.zeros((b, CONV_QKV_W - 1, D_QKV), dt)
    zero_s = jnp.zeros((b, N_HEADS, HEAD_DIM, HEAD_DIM), jnp.float32)
    p_a, p_qkv, p_s, s_a, s_qkv, s_s = [], [], [], [], [], []
    for layer in range(DEPTH):
        wts = (norm_w[layer], w_in[layer], conv_a_w[layer], conv_qkv_w[layer],
               a_log[layer], dt_bias[layer], o_norm_w[layer], w_out[layer])
        hp, ca, cq, st = _layer(hp, zero_a, zero_qkv, zero_s, N_META, *wts)
        p_a.append(ca); p_qkv.append(cq); p_s.append(st)
        hs, ca, cq, st = _layer(hs, state_conv_a[layer], state_conv_qkv[layer], state_delta[layer], 0, *wts)
        s_a.append(ca); s_qkv.append(cq); s_s.append(st)
    y_prompt = _rmsnorm(hp[:, N_META:], final_norm_w)
    y_sample = _rmsnorm(hs, final_norm_w)
    return (y_prompt, y_sample, jnp.stack(p_a), jnp.stack(p_qkv), jnp.stack(p_s),
            jnp.stack(s_a), jnp.stack(s_qkv), jnp.stack(s_s))
```

```python
import contextlib
import numpy as np
import concourse.bass as bass
import concourse.mybir as mybir
from concourse.bass_utils import run_bass_kernel_spmd

F32 = mybir.dt.float32
BF16 = mybir.dt.bfloat16
AF = mybir.ActivationFunctionType
ALU = mybir.AluOpType

D = 2048
DC = 16
NH = 8
HD = 128
DP = 8208
NMETA = 16
EPS = 1e-6
NTB = 512
HB = 4
BIG = 1.0e9

import os as _os
SAME_ENGINE_SYNC = not bool(_os.environ.get("KNOSES"))


class Buf:
    __slots__ = ("name", "w", "r", "nowaw")

    def __init__(self, name="", nowaw=False):
        self.name = name
        self.w = None
        self.r = []
        self.nowaw = nowaw


class Prog:
    ENG = ("pe", "act", "dve", "pool", "sp")

    def __init__(self, nc):
        self.nc = nc
        self.q = {e: [] for e in self.ENG}
        self.n = {e: 0 for e in self.ENG}
        self.seen = {e: {} for e in self.ENG}
        self.dma_total = {}
        self.waited = {e: set() for e in self.ENG}
        self.recs = []
        self.tsets = {}
        self.est_time = 0.0

    def _deps(self, e, reads, writes):
        deps = {}

        def add(t):
            if t is None:
                return
            k, v = t
            if deps.get(k, 0) < v:
                deps[k] = v
        for b in reads:
            add(b.w)
        for b in writes:
            if not b.nowaw:
                add(b.w)
            for t in b.r:
                add(t)
        waits = []
        for k, v in deps.items():
            if k == e and (e == "pe" or not SAME_ENGINE_SYNC):
                continue
            if self.seen[e].get(k, 0) >= v:
                continue
            self.seen[e][k] = v
            waits.append((k, v))
            if k in self.waited:
                self.waited[k].add(v)
        return waits

    def _post(self, t, reads, writes):
        for b in reads:
            if len(b.r) > 24:
                d = {}
                for k, v in b.r:
                    if d.get(k, 0) < v:
                        d[k] = v
                b.r = list(d.items())
            b.r.append(t)
        for b in writes:
            b.w = t
            b.r = []

    def op(self, e, fn, reads=(), writes=(), cost=0.3, tset=0):
        self.recs.append((e, fn, tuple(reads), tuple(writes), None, 1, cost, False))
        if tset:
            self.tsets[len(self.recs) - 1] = tset
        return None

    def dma(self, e, fn, semkey, reads=(), writes=(), n=1, cost=4.0, final=False):
        self.recs.append((e, fn, tuple(reads), tuple(writes), semkey, n, cost, final))
        return None

    def schedule(self, reorder=True):
        import heapq
        recs = self.recs
        N = len(recs)
        last_w = {}
        readers = {}
        deps = [None] * N
        for i, (e, fn, reads, writes, semkey, n, cost, final) in enumerate(recs):
            d = set()
            for b in reads:
                w = last_w.get(id(b))
                if w is not None:
                    d.add(w)
            for b in writes:
                w = last_w.get(id(b))
                if w is not None:
                    d.add(w)
                for r in readers.get(id(b), ()):
                    d.add(r)
            d.discard(i)
            deps[i] = d
            for b in reads:
                readers.setdefault(id(b), []).append(i)
            for b in writes:
                last_w[id(b)] = i
                readers[id(b)] = []
        order = list(range(N))
        if reorder:
            succ = [[] for _ in range(N)]
            indeg = [0] * N
            for i in range(N):
                indeg[i] = len(deps[i])
                for d in deps[i]:
                    succ[d].append(i)
            finish = [0.0] * N
            eng_free = {e: 0.0 for e in self.ENG}
            pending = {e: [] for e in self.ENG}
            avail = {e: [] for e in self.ENG}
            for i in range(N):
                if indeg[i] == 0:
                    heapq.heappush(pending[recs[i][0]], (0.0, i))
            order = []
            cur_tset = [0]
            LOOK = 6000
            nxt_unsched = 0
            done = [False] * N
            while len(order) < N:
                best = None
                while nxt_unsched < N and done[nxt_unsched]:
                    nxt_unsched += 1
                for e in self.ENG:
                    pe_, av = pending[e], avail[e]
                    while pe_ and pe_[0][0] <= eng_free[e]:
                        rt, i = heapq.heappop(pe_)
                        heapq.heappush(av, i)
                    cand = None
                    if av and e == "act" and len(av) > 1 and cur_tset[0]:
                        pick = None
                        for ii in heapq.nsmallest(6, av):
                            ts = self.tsets.get(ii, 0)
                            if ts == 0 or ts == cur_tset[0]:
                                pick = ii
                                break
                        if pick is not None and pick != av[0] and pick - av[0] < 400:
                            av.remove(pick)
                            heapq.heapify(av)
                            top = av[0]
                            heapq.heappush(av, pick)
                            cand = (eng_free[e], pick, e, "pick")
                    if cand is not None:
                        pass
                    elif av:
                        cand = (eng_free[e], av[0], e, True)
                    elif pe_:
                        cand = (pe_[0][0], pe_[0][1], e, False)
                    if cand is None:
                        continue
                    if cand[1] > nxt_unsched + LOOK:
                        cand = (cand[0] + 1e6, cand[1], e, cand[3])
                    if best is None or cand[:2] < best[:2]:
                        best = cand
                st, i, e, from_av = best
                if st >= 1e6:
                    st -= 1e6
                if from_av == "pick":
                    avail[e].remove(i)
                    heapq.heapify(avail[e])
                elif from_av:
                    heapq.heappop(avail[e])
                else:
                    heapq.heappop(pending[e])
                if e == "act" and self.tsets.get(i, 0):
                    cur_tset[0] = self.tsets[i]
                rec = recs[i]
                start = max(st, eng_free[e])
                if rec[4] is not None:
                    eng_free[e] = start + 0.06
                    finish[i] = start + rec[6]
                else:
                    eng_free[e] = start + rec[6]
                    finish[i] = eng_free[e]
                done[i] = True
                order.append(i)
                for s_ in succ[i]:
                    indeg[s_] -= 1
                    if indeg[s_] == 0:
                        es = recs[s_][0]
                        rt = 0.0
                        for d in deps[s_]:
                            lat = 0.06 if recs[d][0] == es and recs[d][4] is None else 0.25
                            rt = max(rt, finish[d] + lat)
                        heapq.heappush(pending[es], (rt, s_))
            self.est_time = max(finish) if N else 0.0
        finals = []
        for i in order:
            e, fn, reads, writes, semkey, n, cost, final = recs[i]
            if semkey is None:
                self._op(e, fn, reads, writes)
            else:
                t = self._dma(e, fn, semkey, reads, writes, n)
                if final:
                    finals.append(t)
        fin = {}
        for (k, v) in finals:
            fin[k] = max(fin.get(k, 0), v)
        self.wait_all("sp", list(fin.items()))

    def _op(self, e, fn, reads=(), writes=()):
        waits = self._deps(e, reads, writes)
        self.n[e] += 1
        t = (e, self.n[e])
        self.q[e].append([waits, fn, t, None])
        self._post(t, reads, writes)
        return t

    def _dma(self, e, fn, semkey, reads=(), writes=(), n=1):
        waits = self._deps(e, reads, writes)
        self.dma_total[semkey] = self.dma_total.get(semkey, 0) + 16 * n
        t = (semkey, self.dma_total[semkey])
        self.q[e].append([waits, fn, t, semkey])
        self._post(t, reads, writes)
        return t

    def wait_all(self, e, tickets):
        waits = []
        for (k, v) in tickets:
            if self.seen[e].get(k, 0) >= v:
                continue
            self.seen[e][k] = v
            waits.append((k, v))
            if k in self.waited:
                self.waited[k].add(v)
        self.q[e].append([waits, None, None, None])

    def emit(self):
        nc = self.nc
        handles = {"pe": "tensor", "act": "scalar", "dve": "vector", "pool": "gpsimd", "sp": "sync"}
        with contextlib.ExitStack() as st:
            sems = {}
            for e in self.ENG:
                sems[e] = st.enter_context(nc.semaphore("s_" + e))
            for k in self.dma_total:
                sems[k] = st.enter_context(nc.semaphore("d_" + str(k)))
            vmap = {}
            for e in self.ENG:
                m = {}
                c = 0
                for i in sorted(self.waited[e]):
                    c += 1
                    m[i] = c
                vmap[e] = m
            waited = self.waited
            import os
            if os.environ.get("KDEBUG"):
                print("est_time_us", self.est_time)
                print("instr counts", self.n, "signalled", {e: len(self.waited[e]) for e in self.ENG},
                      "dma", {k: v // 16 for k, v in self.dma_total.items() if v > 64})
            block = st.enter_context(nc.Block())
            for e in self.ENG:
                def body(eng, q=self.q[e], e=e):
                    for waits, fn, t, semkey in q:
                        for (k, v) in waits:
                            if k in vmap:
                                eng.wait_ge(sems[k], vmap[k][v])
                            else:
                                eng.wait_ge(sems[k], v)
                        if fn is None:
                            continue
                        ins = fn(eng)
                        if semkey is not None:
                            if isinstance(ins, (list, tuple)):
                                for i_ in ins:
                                    i_.then_inc(sems[semkey], 16)
                            else:
                                ins.then_inc(sems[semkey], 16)
                        elif t[1] in waited[e]:
                            ins.then_inc(sems[e], 1)
                getattr(block, handles[e])(body)


class TB:
    def __init__(self, t, name, nsub=1):
        self.t = t
        self.b = Buf(name)
        self.bs = [Buf(name + str(i)) for i in range(nsub)] if nsub > 1 else [self.b]


def build(n_big):
    seq = n_big * NTB
    nc = bass.Bass("TRN2", target_bir_lowering=False)

    def din(name, shape, dt=F32):
        return nc.dram_tensor(name, list(shape), dt, kind="ExternalInput").ap()

    def dout(name, shape, dt=F32):
        return nc.dram_tensor(name, list(shape), dt, kind="ExternalOutput").ap()

    xp = din("xp", [max(seq, 128), D])
    xs = din("xs", [64, D])
    sca = din("sca", [128, 2 * 8 * 2])
    scq = din("scq", [128, 2 * 24 * 3])
    sdl = din("sdl", [2, NH, HD, HD])
    meta = din("meta", [NMETA, D])
    normw = din("normw", [128, 2 * DC])
    w_in = din("w_in", [2, D, DP])
    caw = din("caw", [128, 2 * 8 * 3])
    cqw = din("cqw", [128, 2 * 24 * 4])
    alog = din("alog", [8, 2])
    dtb = din("dtb", [8, 2])
    onw = din("onw", [128, 2])
    w_out = din("w_out", [2, D, D])
    fnw = din("fnw", [128, DC])

    yp = dout("yp", [max(seq, 128), D])
    ys = dout("ys", [64, D])
    o_pca = dout("pca", [128, 2 * 8 * 2])
    o_pcq = dout("pcq", [128, 2 * 24 * 3])
    o_pdl = dout("pdl", [2, NH, HD, HD])
    o_sca = dout("sca_o", [128, 2 * 8 * 2])
    o_scq = dout("scq_o", [128, 2 * 24 * 3])
    o_sdl = dout("sdl_o", [2, NH, HD, HD])

    NGI = 64
    wsc_in = nc.dram_tensor("wsc_in", [2, NGI, 128, DC * 128], BF16, kind="Internal").ap()
    wsc_bd = nc.dram_tensor("wsc_bd", [2, 2, 128, DC * 8], BF16, kind="Internal").ap()
    wsc_out = nc.dram_tensor("wsc_out", [2, DC, 128, DC * 128], BF16, kind="Internal").ap()

    KDBG = bool(_os.environ.get("KDBG"))
    if KDBG:
        d_posm = dout("d_posm", [128, 128]); d_negm = dout("d_negm", [128, 128])
        d_L = dout("d_L", [128, 128], BF16); d_Ed = dout("d_Ed", [128, 128]); d_P = dout("d_P", [128, 256], BF16)
        d_sl = dout("d_sl", [128, 128])
    dbg_done = {}
    P = Prog(nc)
    st = contextlib.ExitStack()
    with st:
        def sb(name, shape, dt=F32, nsub=1):
            return TB(st.enter_context(nc.sbuf_tensor(name, list(shape), dt)), name, nsub)

        hT = sb("hT", [128, DC, NTB], F32, DC)
        xn = sb("xn", [128, DC, NTB], BF16, DC)
        yT = sb("yT", [128, DC, NTB], BF16, DC)
        NW = 6
        wt = [sb("wt%d" % i, [128, DC, 128], BF16) for i in range(NW)]
        stg = sb("stg", [128, D], F32)
        oT = TB(stg.t[:, :].rearrange("p (a b) -> p a b", a=HB), "oT")
        oT.b = stg.b
        oT.bs = [stg.b]
        NSC = 12
        SC = [sb("sc%d" % i, [128, NTB + 4], F32) for i in range(NSC)]
        sqb = sb("sqb", [128, NTB], BF16)
        qTb = sb("qTb", [128, NTB], BF16)
        kTb = sb("kTb", [128, NTB], BF16)
        vbf = sb("vbf", [128, NTB], BF16)
        NBLK = NTB // 128
        vb = sb("vb", [128, NBLK, 128], BF16)
        kbg = sb("kbg", [128, NBLK, 128], BF16)
        qgT = [sb("qgT%d" % i, [128, NTB], BF16) for i in range(NH)]
        qkmT = [sb("qkmT%d" % i, [128, NBLK, 128], BF16) for i in range(HB)]
        kdec = [sb("kdec%d" % i, [128, NBLK, 128], BF16) for i in range(HB)]
        wTz = [sb("wTz%d" % i, [128, NBLK, 2, 128], BF16) for i in range(HB)]
        uu = sb("uu", [128, NBLK, HB, 128], BF16)
        szb = [sb("szb%d" % i, [128, NTB], BF16) for i in range(NH)]
        egl = sb("egl", [128, NH, NTB // 64], F32)
        vn = sb("vn", [128, HB, 128], BF16)
        Dm = sb("Dm", [128, NBLK, 128], F32)
        EdB = sb("EdB", [128, NBLK, 128], F32)
        EdTB = sb("EdTB", [128, NBLK, 128], F32)
        T1 = sb("T1", [128, NBLK, 128], BF16)
        Lk = [sb("Lk%d" % i, [128, NBLK, 128], BF16) for i in range(2)]
        PQ = [sb("PQ%d" % i, [128, NBLK, 256], BF16) for i in range(2)]
        ident8 = sb("ident8", [128, NBLK, 128], BF16)
        S = [sb("S%d" % l, [128, NH, 128], F32, NH) for l in range(2)]
        Sbf = sb("Sbf", [128, NH, 128], BF16, NH)
        tailA = [sb("tailA%d" % l, [128, 8, 2], F32) for l in range(2)]
        tailQ = [sb("tailQ%d" % l, [128, 24, 3], F32) for l in range(2)]
        r_beta = sb("r_beta", [8, NTB], F32)
        r_g = sb("r_g", [8, NTB], F32)
        r_gc = sb("r_gc", [8, NTB], F32)
        NB8 = NBLK * 8
        tm_gc = sb("tm_gc", [128, NB8], F32)
        tm_beta = sb("tm_beta", [128, NB8], F32)
        tm_gl = sb("tm_gl", [128, NB8], F32)
        tm_bg = sb("tm_bg", [128, NB8], F32)
        tm_dk = sb("tm_dk", [128, NB8], F32)
        ident = sb("ident", [128, 128], F32)
        identb = sb("identb", [128, 128], BF16)
        onesb = sb("onesb", [128, 128], BF16)
        onesf = sb("onesf", [128, 128], F32)
        negm = sb("negm", [128, 1, 128], F32)
        posm = sb("posm", [128, 1, 128], F32)
        sel = sb("sel", [8, NH, 128], F32)
        sellast = sb("sellast", [128, 128], F32)
        rmask = sb("rmask", [8, NTB], F32)
        c_normw = sb("c_normw", [128, 2 * DC], F32)
        c_fnw = sb("c_fnw", [128, DC], F32)
        c_caw = sb("c_caw", [128, 2 * 8 * 3], F32)
        c_cqw = sb("c_cqw", [128, 2 * 24 * 4], F32)
        c_onw = sb("c_onw", [128, 2], F32)
        c_alog = sb("c_alog", [8, 2], F32)
        c_dtb = sb("c_dtb", [8, 2], F32)
        c_nA = sb("c_nA", [8, 2], F32)
        wbd = [sb("wbd%d" % l, [128, 2, DC, 8], BF16) for l in range(2)]

        NPJ = 3
        pj = [TB(st.enter_context(nc.psum_tensor("pj%d" % i, [128, 512], F32)), "pj%d" % i) for i in range(NPJ)]
        NPG = 5
        pg = [TB(st.enter_context(nc.psum_tensor("pg%d" % i, [128, 512], F32)), "pg%d" % i) for i in range(NPG)]
        ctr = {"pj": 0, "pg": 0, "w": 0, "sc": 0, "ch": 0}

        def PJ():
            ctr["pj"] += 1
            return pj[ctr["pj"] % NPJ]

        def PG():
            ctr["pg"] += 1
            return pg[ctr["pg"] % NPG]

        def _fs(ap):
            n = 1
            for d_ in ap.shape[1:]:
                n *= int(d_)
            return n

        def _ecost(eng, n):
            if eng == "dve":
                return 0.08 + n / 960.0
            if eng == "act":
                return 0.22 + n / 1100.0
            return 0.25 + n / 480.0

        def MM(out, lhsT, rhs, r, w, start=True, stop=True):
            c = 0.035 + (4 if rhs.dtype == F32 else 1) * _fs(rhs) / 1900.0
            return P.op("pe", lambda e: e.matmul(out, lhsT=lhsT, rhs=rhs, start=start, stop=stop), r, w, cost=c)

        def TR(out, in_, idn, r, w):
            c = 0.06 + (4 if in_.dtype == F32 else 1) * _fs(idn) / 1900.0
            return P.op("pe", lambda e: e.transpose(out, in_, idn), r, w, cost=c)

        def ACT(out, in_, func, r, w, bias=None, scale=None):
            kw = {}
            if bias is not None:
                kw["bias"] = bias
            if scale is not None:
                kw["scale"] = scale
            ts = {AF.Silu: 1, AF.Exp: 2, AF.Ln: 2, AF.Sigmoid: 3}.get(func, 0)
            return P.op("act", lambda e: e.activation(out=out, in_=in_, func=func, **kw), r, w,
                        cost=_ecost("act", _fs(out)), tset=ts)

        def TS(eng, out, in0, s1, op0, r, w, s2=None, op1=None):
            c = _ecost(eng, _fs(out))
            if op1 is None:
                return P.op(eng, lambda e: e.tensor_scalar(out=out, in0=in0, scalar1=s1, scalar2=None, op0=op0), r, w, cost=c)
            return P.op(eng, lambda e: e.tensor_scalar(out=out, in0=in0, scalar1=s1, scalar2=s2, op0=op0, op1=op1), r, w, cost=c)

        def STT(out, in0, scalar, in1, op0, op1, r, w):
            return P.op("dve", lambda e: e.scalar_tensor_tensor(out=out, in0=in0, scalar=scalar, in1=in1, op0=op0, op1=op1),
                        r, w, cost=_ecost("dve", _fs(out)))

        def TT(eng, out, in0, in1, op, r, w):
            return P.op(eng, lambda e: e.tensor_tensor(out=out, in0=in0, in1=in1, op=op), r, w, cost=_ecost(eng, _fs(out)))

        def CP(eng, out, in_, r, w):
            c = _ecost(eng, _fs(out))
            if eng == "act":
                return P.op(eng, lambda e: e.activation(out=out, in_=in_, func=AF.Copy), r, w, cost=c)
            return P.op(eng, lambda e: e.tensor_copy(out=out, in_=in_), r, w, cost=c)

        def RECIP(out, in_, r, w):
            return P.op("dve", lambda e: e.reciprocal(out=out, in_=in_), r, w, cost=0.1 + _fs(out) / 170.0)

        def MS(eng, ap, val, w):
            return P.op(eng, lambda e: e.memset(ap, val), (), w, cost=0.1 + _fs(ap) / 1000.0)

        def AFS(out, in_, pattern, cmp, fill, base, cm, r, w):
            return P.op("pool", lambda e: e.affine_select(out=out, in_=in_, pattern=pattern, compare_op=cmp,
                                                         fill=fill, base=base, channel_multiplier=cm), r, w, cost=0.4)

        def DMA(eng, out, in_, key, r, w, slow=False, final=False):
            nbytes = 1
            for d_ in out.shape:
                nbytes *= int(d_)
            nbytes *= (4 if out.dtype == F32 else 2)
            c = 2.0 + nbytes / 150e3
            if slow:
                return P.dma(eng, lambda e: e.dma_start(out=out, in_=in_, allow_slow_non_contiguous=True), key, r, w,
                             cost=c, final=final)
            return P.dma(eng, lambda e: e.dma_start(out=out, in_=in_), key, r, w, cost=c, final=final)

        cb = Buf("consts")
        for i, (dst, src) in enumerate([(c_normw, normw), (c_fnw, fnw), (c_caw, caw), (c_cqw, cqw), (c_onw, onw),
                                        (c_alog, alog), (c_dtb, dtb)]):
            DMA("sp", dst.t[:], src, "cst%d" % i, (), [dst.b])
        MS("pool", onesf.t[:], 1.0, [onesf.b])
        MS("pool", onesb.t[:], 1.0, [onesb.b])
        AFS(ident.t[:], onesf.t[:, 0:128], [[-1, 128]], ALU.is_equal, 0.0, 0, 1, [onesf.b], [ident.b])
        CP("pool", identb.t[:], ident.t[:], [ident.b], [identb.b])
        MS("pool", negm.t[:], 0.0, [negm.b])
        MS("pool", posm.t[:], 0.0, [posm.b])
        for cblk in range(2):
            cs = slice(cblk * 64, (cblk + 1) * 64)
            AFS(negm.t[:, 0, cs], negm.t[:, 0, cs], [[1, 64]], ALU.is_ge, -BIG, cblk * 64, -1, [negm.b], [negm.b])
            AFS(negm.t[:, 0, cs], negm.t[:, 0, cs], [[0, 64]], ALU.is_ge, -BIG, -cblk * 64, 1, [negm.b], [negm.b])
            AFS(posm.t[:, 0, cs], posm.t[:, 0, cs], [[-1, 64]], ALU.is_ge, BIG, -cblk * 64 - 1, 1, [posm.b], [posm.b])
            AFS(posm.t[:, 0, cs], posm.t[:, 0, cs], [[0, 64]], ALU.is_ge, BIG, (cblk + 1) * 64 - 1, -1, [posm.b], [posm.b])
        for blk in range(NBLK):
            CP("pool", ident8.t[:, blk, :], ident.t[:, :], [ident.b], [ident8.b])
        for h in range(NH):
            AFS(sel.t[:, h, :], onesf.t[0:8, 0:128], [[0, 128]], ALU.is_equal, 0.0, -h, 1, [onesf.b], [sel.b])
        MS("pool", sellast.t[:], 1.0, [sellast.b])
        for cblk in range(2):
            cs = slice(cblk * 64, (cblk + 1) * 64)
            AFS(sellast.t[:, cs], sellast.t[:, cs], [[0, 64]], ALU.is_equal, 0.0, -(cblk * 64 + 63), 1,
                [sellast.b], [sellast.b])
        MS("pool", rmask.t[:], 1.0, [rmask.b])
        for c in range(NTB // 64):
            MS("pool", rmask.t[:, c * 64:c * 64 + 1], 0.0, [rmask.b])
        for i in range(HB):
            MS("pool", wTz[i].t[:], 0.0, [wTz[i].b])
        ACT(c_nA.t[:], c_alog.t[:], AF.Exp, [c_alog.b], [c_nA.b])
        TS("dve", c_nA.t[:], c_nA.t[:], -1.0, ALU.mult, [c_nA.b], [c_nA.b])

        wph = {}
        for l in range(2):
            src = w_in[l].rearrange("(c p) n -> p c n", p=128)
            srco = w_out[l].rearrange("(c p) n -> p c n", p=128)
            phases = []
            phases.append([("bd", j) for j in range(2)] + [("in", k * 8 + cc) for cc in range(0, 4) for k in range(4)])
            phases.append([("in", k * 8 + cc) for cc in range(4, 8) for k in range(4)])
            for h0 in range(0, 8, 2):
                phases.append([("in", 32 + k * 8 + h) for h in (h0, h0 + 1) for k in range(4)])
            phases.append([("out", g) for g in range(0, 8)])
            phases.append([("out", g) for g in range(8, 16)])
            for pi, ph in enumerate(phases):
                key = "wc%d_%d" % (l, pi)
                pb = Buf(key, nowaw=True)
                for (kind, g) in ph:
                    if kind == "in":
                        DMA("pool", wsc_in[l, g].rearrange("p (c n) -> p c n", c=DC), src[:, :, g * 128:(g + 1) * 128],
                            key, (), [pb])
                        wph[("in", l, g)] = pb
                    elif kind == "bd":
                        DMA("pool", wsc_bd[l, g].rearrange("p (c n) -> p c n", c=DC),
                            src[:, :, 8192 + g * 8:8192 + (g + 1) * 8], key, (), [pb], slow=True)
                        wph[("bd", l)] = pb
                    else:
                        DMA("pool", wsc_out[l, g].rearrange("p (c n) -> p c n", c=DC),
                            srco[:, :, g * 128:(g + 1) * 128], key, (), [pb])
                        wph[("out", l, g)] = pb
        for l in range(2):
            for j in range(2):
                DMA("sp", wbd[l].t[:, j, :, :], wsc_bd[l, j].rearrange("p (c n) -> p c n", c=DC), "cstb%d%d" % (l, j),
                    [wph[("bd", l)]], [wbd[l].b])

        def load_w(src_ap, pb):
            ctr["w"] += 1
            i = ctr["w"] % NW
            DMA("sp", wt[i].t[:], src_ap.rearrange("p (c n) -> p c n", c=DC), "w%d" % i, [pb], [wt[i].b])
            return wt[i]

        def proj(wtile, NT, M=128, lhs=None):
            ps = PJ()
            for dc in range(DC):
                l_ap = wtile.t[:, dc, 0:M] if lhs is None else lhs(dc)
                MM(ps.t[0:M, 0:NT], l_ap, xn.t[:, dc, 0:NT], [wtile.b, xn.bs[dc]], [ps.b],
                   start=(dc == 0), stop=(dc == DC - 1))
            return ps

        def scr():
            ctr["sc"] += 1
            return SC[ctr["sc"] % NSC]

        def rmsnorm_fm(NT, wcol, out_fn, out_bufs, wb):
            rstd = scr()
            ps = PG()
            for dc in range(DC):
                ACT(yT.t[:, dc, 0:NT], hT.t[:, dc, 0:NT], AF.Square, [hT.bs[dc]], [yT.bs[dc]])
            for dc in range(DC):
                MM(ps.t[:, 0:NT], onesb.t[:], yT.t[:, dc, 0:NT], [onesb.b, yT.bs[dc]], [ps.b],
                   start=(dc == 0), stop=(dc == DC - 1))
            ACT(rstd.t[:, 0:NT], ps.t[:, 0:NT], AF.Ln, [ps.b], [rstd.b], bias=EPS, scale=1.0 / D)
            ACT(rstd.t[:, 0:NT], rstd.t[:, 0:NT], AF.Exp, [rstd.b], [rstd.b], scale=-0.5)
            for dc in range(DC):
                STT(out_fn(dc), hT.t[:, dc, 0:NT], wcol(dc), rstd.t[:, 0:NT], ALU.mult, ALU.mult,
                    [hT.bs[dc], rstd.b, wb], [out_bufs[dc]])


        def v3(ap2, n):
            return ap2.rearrange("p (b i) -> p b i", i=n)

        def intra(l, NT, h, hi, gcb):
            BS = min(128, NT)
            nblk = NT // BS
            n8 = nblk * 8
            W = nblk * BS

            def tmb(tm, n):
                return tm.t[0:BS, 0:n8].rearrange("p (b e) -> p b e", e=8)[:, :, h:h + 1].broadcast_to([BS, nblk, n])

            def blkv(ap2, n):
                return ap2.rearrange("p (b i) -> p b i", i=n)
            TT("dve", Dm.t[0:BS, 0:nblk, 0:BS], blkv(gcb.t[0:BS, 0:NT], BS), tmb(tm_gc, BS), ALU.subtract,
               [gcb.b, tm_gc.b], [Dm.b])
            TT("dve", EdTB.t[0:BS, 0:nblk, 0:BS], Dm.t[0:BS, 0:nblk, 0:BS],
               negm.t[0:BS, 0:1, 0:BS].broadcast_to([BS, nblk, BS]), ALU.min, [Dm.b, negm.b], [EdTB.b])
            TT("dve", EdB.t[0:BS, 0:nblk, 0:BS], Dm.t[0:BS, 0:nblk, 0:BS],
               posm.t[0:BS, 0:1, 0:BS].broadcast_to([BS, nblk, BS]), ALU.max, [Dm.b, posm.b], [EdB.b])
            ACT(EdTB.t[0:BS, 0:nblk, 0:BS], EdTB.t[0:BS, 0:nblk, 0:BS], AF.Exp, [EdTB.b], [EdTB.b])
            ACT(EdB.t[0:BS, 0:nblk, 0:BS], EdB.t[0:BS, 0:nblk, 0:BS], AF.Exp, [EdB.b], [EdB.b], scale=-1.0)
            pk = PG()
            pk_b = pk.t[:].bitcast(BF16)
            for blk in range(nblk):
                TR(pk_b[0:BS, blk * 128:(blk + 1) * 128], kTb.t[:, blk * BS:(blk + 1) * BS], identb.t[:],
                   [kTb.b, identb.b], [pk.b])
            pv = PG()
            pv_b = pv.t[:].bitcast(BF16)
            for blk in range(nblk):
                TR(pv_b[0:BS, blk * 128:(blk + 1) * 128], vbf.t[:, blk * BS:(blk + 1) * BS], identb.t[:],
                   [vbf.b, identb.b], [pv.b])
            TT("dve", kbg.t[0:BS, 0:nblk, :], blkv(pk_b[0:BS, 0:nblk * 128], 128), tmb(tm_bg, 128), ALU.mult,
               [pk.b, tm_bg.b], [kbg.b])
            TT("dve", kdec[hi].t[0:BS, 0:nblk, :], blkv(pk_b[0:BS, 0:nblk * 128], 128), tmb(tm_dk, 128), ALU.mult,
               [pk.b, tm_dk.b], [kdec[hi].b])
            TT("dve", vb.t[0:BS, 0:nblk, :], blkv(pv_b[0:BS, 0:nblk * 128], 128), tmb(tm_beta, 128), ALU.mult,
               [pv.b, tm_beta.b], [vb.b])
            pkk = PG()
            for blk in range(nblk):
                bsl = slice(blk * BS, (blk + 1) * BS)
                MM(pkk.t[0:BS, bsl], kTb.t[:, bsl], kTb.t[:, bsl], [kTb.b], [pkk.b])
            pqk = PG()
            for blk in range(nblk):
                bsl = slice(blk * BS, (blk + 1) * BS)
                MM(pqk.t[0:BS, bsl], kTb.t[:, bsl], qTb.t[:, bsl], [kTb.b, qTb.b], [pqk.b])
            TT("dve", T1.t[0:BS, 0:nblk, 0:BS], blkv(pkk.t[0:BS, 0:W], BS), tmb(tm_beta, BS), ALU.mult,
               [pkk.b, tm_beta.b], [T1.b])
            cur = 0
            TT("pool", Lk[cur].t[0:BS, 0:nblk, 0:BS], T1.t[0:BS, 0:nblk, 0:BS], EdB.t[0:BS, 0:nblk, 0:BS], ALU.mult,
               [T1.b, EdB.b], [Lk[cur].b])
            TT("dve", qkmT[hi].t[0:BS, 0:nblk, 0:BS], blkv(pqk.t[0:BS, 0:W], BS), EdTB.t[0:BS, 0:nblk, 0:BS], ALU.mult,
               [pqk.b, EdTB.b], [qkmT[hi].b])
            plt = PG()
            plt_b = plt.t[:].bitcast(BF16)
            for blk in range(nblk):
                TR(plt_b[0:BS, blk * BS:(blk + 1) * BS], Lk[cur].t[0:BS, blk, 0:BS], identb.t[0:BS, 0:BS],
                   [Lk[cur].b, identb.b], [plt.b])
            CP("dve", PQ[cur].t[0:BS, 0:nblk, BS:2 * BS], blkv(plt_b[0:BS, 0:W], BS), [plt.b], [PQ[cur].b])
            TT("dve", PQ[cur].t[0:BS, 0:nblk, 0:BS], ident8.t[0:BS, 0:nblk, 0:BS], blkv(plt_b[0:BS, 0:W], BS),
               ALU.subtract, [ident8.b, plt.b], [PQ[cur].b])
            bpb = max(1, 512 // (2 * BS))
            for s_ in range(6):
                first = (s_ == 0)
                last = (s_ == 5)
                nxt = 1 - cur
                cL, cPQ, nPQ = Lk[cur], PQ[cur], PQ[nxt]
                if first:
                    p1 = PG()
                    for blk in range(nblk):
                        MM(p1.t[0:BS, blk * BS:(blk + 1) * BS], cL.t[0:BS, blk, 0:BS], cPQ.t[0:BS, blk, BS:2 * BS],
                           [cL.b, cPQ.b], [p1.b])
                    CP("dve", nPQ.t[0:BS, 0:nblk, BS:2 * BS], blkv(p1.t[0:BS, 0:W], BS), [p1.b], [nPQ.b])
                    CP("dve", nPQ.t[0:BS, 0:nblk, 0:BS], cPQ.t[0:BS, 0:nblk, 0:BS], [cPQ.b], [nPQ.b])
                elif not last:
                    for b0 in range(0, nblk, bpb):
                        nb_ = min(bpb, nblk - b0)
                        p01 = PG()
                        for blk in range(b0, b0 + nb_):
                            o_ = (blk - b0) * 2 * BS
                            MM(p01.t[0:BS, o_:o_ + 2 * BS], cL.t[0:BS, blk, 0:BS], cPQ.t[0:BS, blk, 0:2 * BS],
                               [cL.b, cPQ.b], [p01.b])
                        pv3 = blkv(p01.t[0:BS, 0:nb_ * 2 * BS], 2 * BS)
                        TT("dve", nPQ.t[0:BS, b0:b0 + nb_, 0:BS], pv3[:, :, 0:BS], cPQ.t[0:BS, b0:b0 + nb_, 0:BS], ALU.add,
                           [p01.b, cPQ.b], [nPQ.b])
                        CP("dve", nPQ.t[0:BS, b0:b0 + nb_, BS:2 * BS], pv3[:, :, BS:2 * BS], [p01.b], [nPQ.b])
                else:
                    p0 = PG()
                    for blk in range(nblk):
                        MM(p0.t[0:BS, blk * BS:(blk + 1) * BS], cL.t[0:BS, blk, 0:BS], cPQ.t[0:BS, blk, 0:BS],
                           [cL.b, cPQ.b], [p0.b])
                    TT("dve", nPQ.t[0:BS, 0:nblk, 0:BS], blkv(p0.t[0:BS, 0:W], BS), cPQ.t[0:BS, 0:nblk, 0:BS], ALU.add,
                       [p0.b, cPQ.b], [nPQ.b])
                if not last:
                    p2 = PG()
                    for blk in range(nblk):
                        MM(p2.t[0:BS, blk * BS:(blk + 1) * BS], cPQ.t[0:BS, blk, BS:2 * BS], cL.t[0:BS, blk, 0:BS],
                           [cL.b, cPQ.b], [p2.b])
                    CP("act", Lk[nxt].t[0:BS, 0:nblk, 0:BS], blkv(p2.t[0:BS, 0:W], BS), [p2.b], [Lk[nxt].b])
                cur = nxt
            Tt = PQ[cur]
            pw = PG()
            for blk in range(nblk):
                MM(pw.t[:, blk * BS:(blk + 1) * BS], kbg.t[0:BS, blk, :], Tt.t[0:BS, blk, 0:BS], [kbg.b, Tt.b], [pw.b])
            pw3 = blkv(pw.t[:, 0:W], BS)
            CP("act", wTz[hi].t[:, 0:nblk, 0, 0:64], pw3[:, :, 0:64], [pw.b], [wTz[hi].b])
            if BS == 128:
                CP("act", wTz[hi].t[:, 0:nblk, 1, 64:128], pw3[:, :, 64:128], [pw.b], [wTz[hi].b])
            pu = PG()
            for blk in range(nblk):
                MM(pu.t[0:BS, blk * 128:(blk + 1) * 128], Tt.t[0:BS, blk, 0:BS], vb.t[0:BS, blk, :],
                   [Tt.b, vb.b], [pu.b])
            CP("dve", uu.t[0:BS, 0:nblk, hi, :], blkv(pu.t[0:BS, 0:nblk * 128], 128), [pu.b], [uu.b])

        import os
        KSTAGE = int(os.environ.get("KSTAGE", "99"))
        KSUB = int(os.environ.get("KSUB", "99"))

        out_tickets = []
        def layer(l, NT):
            BS = min(128, NT)
            nblk = NT // BS
            nch = NT // 64
            rmsnorm_fm(NT, lambda dc: c_normw.t[:, l * DC + dc:l * DC + dc + 1],
                       lambda dc: xn.t[:, dc, 0:NT], xn.bs, c_normw.b)

            psb = proj(wbd[l], NT, M=8, lhs=lambda dc: wbd[l].t[:, 0, dc, :])
            ACT(r_beta.t[:, 0:NT], psb.t[0:8, 0:NT], AF.Sigmoid, [psb.b], [r_beta.b])
            psa = proj(wbd[l], NT, M=8, lhs=lambda dc: wbd[l].t[:, 1, dc, :])
            ACT(r_g.t[:, 0:NT], psa.t[0:8, 0:NT], AF.Exp, [psa.b, c_dtb.b], [r_g.b], bias=c_dtb.t[:, l:l + 1])
            ACT(r_g.t[:, 0:NT], r_g.t[:, 0:NT], AF.Ln, [r_g.b], [r_g.b], bias=1.0)
            TS("dve", r_g.t[:, 0:NT], r_g.t[:, 0:NT], c_nA.t[:, l:l + 1], ALU.mult, [r_g.b, c_nA.b], [r_g.b])
            P.op("dve", lambda e: e.tensor_tensor_scan(out=r_gc.t[:, 0:NT], data0=rmask.t[:, 0:NT],
                                                       data1=r_g.t[:, 0:NT], initial=0.0,
                                                       op0=ALU.mult, op1=ALU.add),
                 [rmask.b, r_g.b], [r_gc.b])
            pst = PG()
            for blk in range(nblk):
                TR(pst.t[0:BS, blk * 8:blk * 8 + 8], r_gc.t[:, blk * BS:(blk + 1) * BS], ident.t[0:8, 0:8],
                   [r_gc.b, ident.b], [pst.b])
                TR(pst.t[0:BS, 128 + blk * 8:128 + blk * 8 + 8], r_beta.t[:, blk * BS:(blk + 1) * BS],
                   ident.t[0:8, 0:8], [r_beta.b, ident.b], [pst.b])
            n8 = nblk * 8
            CP("dve", tm_gc.t[0:BS, 0:n8], pst.t[0:BS, 0:n8], [pst.b], [tm_gc.b])
            CP("dve", tm_beta.t[0:BS, 0:n8], pst.t[0:BS, 128:128 + n8], [pst.b], [tm_beta.b])
            psl = PG()
            MM(psl.t[0:BS, 0:n8], sellast.t[0:BS, 0:BS], tm_gc.t[0:BS, 0:n8], [sellast.b, tm_gc.b], [psl.b])
            TT("dve", tm_gl.t[0:BS, 0:n8], psl.t[0:BS, 0:n8], tm_gc.t[0:BS, 0:n8], ALU.subtract,
               [psl.b, tm_gc.b], [tm_gl.b])
            ACT(tm_dk.t[0:BS, 0:n8], tm_gl.t[0:BS, 0:n8], AF.Exp, [tm_gl.b], [tm_dk.b])
            ACT(tm_bg.t[0:BS, 0:n8], tm_gc.t[0:BS, 0:n8], AF.Exp, [tm_gc.b], [tm_bg.b])
            TT("dve", tm_bg.t[0:BS, 0:n8], tm_bg.t[0:BS, 0:n8], tm_beta.t[0:BS, 0:n8], ALU.mult,
               [tm_bg.b, tm_beta.b], [tm_bg.b])

            for hb in range(NH // HB if KSTAGE >= 4 else 0):
                for hi in range(HB):
                    h = hb * HB + hi
                    wq = load_w(wsc_in[l, 32 + h], wph[("in", l, 32 + h)])
                    wk = load_w(wsc_in[l, 40 + h], wph[("in", l, 40 + h)])
                    wv = load_w(wsc_in[l, 48 + h], wph[("in", l, 48 + h)])
                    wz = load_w(wsc_in[l, 56 + h], wph[("in", l, 56 + h)])
                    cs = []
                    for i3, wti in enumerate((wq, wk, wv)):
                        ps = proj(wti, NT)
                        ch = i3 * 8 + h
                        pre = scr()
                        acc = scr()
                        CP("pool", pre.t[:, 0:3], tailQ[l].t[:, ch, :], [tailQ[l].b], [pre.b])
                        ACT(pre.t[:, 3:3 + NT], ps.t[:, 0:NT], AF.Copy, [ps.b], [pre.b])
                        CP("pool", tailQ[l].t[:, ch, :], pre.t[:, NT:NT + 3], [pre.b], [tailQ[l].b])
                        w0 = l * 96 + ch * 4
                        TS("dve", acc.t[:, 0:NT], pre.t[:, 0:NT], c_cqw.t[:, w0:w0 + 1], ALU.mult,
                           [pre.b, c_cqw.b], [acc.b])
                        for j in (1, 2, 3):
                            STT(acc.t[:, 0:NT], pre.t[:, j:j + NT], c_cqw.t[:, w0 + j:w0 + j + 1], acc.t[:, 0:NT],
                                ALU.mult, ALU.add, [pre.b, c_cqw.b, acc.b], [acc.b])
                        ACT(acc.t[:, 0:NT], acc.t[:, 0:NT], AF.Silu, [acc.b], [acc.b])
                        cs.append(acc)
                    qc, kc, vc = cs
                    psz = proj(wz, NT)
                    ACT(szb[h].t[:, 0:NT], psz.t[:, 0:NT], AF.Silu, [psz.b], [szb[h].b])
                    for (src, dstT, scale) in ((qc, qTb, HD ** -0.5), (kc, kTb, 1.0)):
                        ACT(sqb.t[:, 0:NT], src.t[:, 0:NT], AF.Square, [src.b], [sqb.b])
                        psn = PG()
                        MM(psn.t[:, 0:NT], onesb.t[:], sqb.t[:, 0:NT], [onesb.b, sqb.b], [psn.b])
                        rn = scr()
                        ACT(rn.t[:, 0:NT], psn.t[:, 0:NT], AF.Ln, [psn.b], [rn.b], bias=EPS, scale=1.0)
                        ACT(rn.t[:, 0:NT], rn.t[:, 0:NT], AF.Exp, [rn.b], [rn.b], scale=-0.5)
                        STT(dstT.t[:, 0:NT], src.t[:, 0:NT], scale, rn.t[:, 0:NT], ALU.mult, ALU.mult,
                            [src.b, rn.b], [dstT.b])
                    CP("pool", vbf.t[:, 0:NT], vc.t[:, 0:NT], [vc.b], [vbf.b])
                    psg = PG()
                    MM(psg.t[:, 0:NT], sel.t[:, h, :], r_gc.t[:, 0:NT], [sel.b, r_gc.b], [psg.b])
                    gcb = scr()
                    egcb = scr()
                    ACT(gcb.t[:, 0:NT], psg.t[:, 0:NT], AF.Copy, [psg.b], [gcb.b])
                    ACT(egcb.t[:, 0:NT], psg.t[:, 0:NT], AF.Exp, [psg.b], [egcb.b])
                    CP("pool", egl.t[:, h, 0:nch], egcb.t[:, 0:NT].rearrange("p (c t) -> p c t", t=64)[:, :, 63],
                       [egcb.b], [egl.b])
                    TT("dve", qgT[h].t[:, 0:NT], qTb.t[:, 0:NT], egcb.t[:, 0:NT], ALU.mult, [qTb.b, egcb.b],
                       [qgT[h].b])
                    if KSTAGE >= 5:
                        intra(l, NT, h, hi, gcb)

                hs = [hb * HB + hi for hi in range(HB)]
                for hi, h in enumerate(hs):
                    CP("pool", Sbf.t[:, h, :], S[l].t[:, h, :], [S[l].bs[h]], [Sbf.bs[h]])
                for c in range(nch if KSTAGE >= 6 else 0):
                    blk = (c * 64) // BS
                    par = c % (BS // 64)
                    rows = slice(par * 64, par * 64 + 64)
                    csl = slice(c * 64, c * 64 + 64)
                    pw_ = PG()
                    for hi, h in enumerate(hs):
                        MM(pw_.t[0:BS, hi * 128:(hi + 1) * 128], wTz[hi].t[:, blk, par, 0:BS], Sbf.t[:, h, :],
                           [wTz[hi].b, Sbf.bs[h]], [pw_.b])
                    TT("dve", vn.t[rows, :, :], uu.t[rows, blk, :, :],
                       pw_.t[rows, :].rearrange("p (a b) -> p a b", a=HB), ALU.subtract,
                       [uu.b, pw_.b], [vn.b])
                    po = PG()
                    pS = PG()
                    for hi, h in enumerate(hs):
                        MM(po.t[:, hi * 64:(hi + 1) * 64], Sbf.t[:, h, :], qgT[h].t[:, csl],
                           [Sbf.bs[h], qgT[h].b], [po.b], start=True, stop=False)
                        MM(po.t[:, hi * 64:(hi + 1) * 64], vn.t[rows, hi, :], qkmT[hi].t[rows, blk, par * 64:par * 64 + 64],
                           [vn.b, qkmT[hi].b], [po.b], start=False, stop=True)
                        MM(pS.t[:, hi * 128:(hi + 1) * 128], kdec[hi].t[rows, blk, :], vn.t[rows, hi, :],
                           [kdec[hi].b, vn.b], [pS.b])
                    CP("act", oT.t[:, :, csl], po.t[:, 0:HB * 64].rearrange("p (a b) -> p a b", a=HB), [po.b], [oT.b])
                    for hi, h in enumerate(hs):
                        STT(S[l].t[:, h, :], S[l].t[:, h, :], egl.t[:, h, c:c + 1], pS.t[:, hi * 128:(hi + 1) * 128],
                            ALU.mult, ALU.add, [S[l].bs[h], egl.b, pS.b], [S[l].bs[h]])
                        CP("pool", Sbf.t[:, h, :], S[l].t[:, h, :], [S[l].bs[h]], [Sbf.bs[h]])

                for hi, h in enumerate(hs if KSTAGE >= 7 else []):
                    ACT(sqb.t[:, 0:NT], oT.t[:, hi, 0:NT], AF.Square, [oT.b], [sqb.b])
                    psn = PG()
                    MM(psn.t[:, 0:NT], onesb.t[:], sqb.t[:, 0:NT], [onesb.b, sqb.b], [psn.b])
                    rn = scr()
                    ACT(rn.t[:, 0:NT], psn.t[:, 0:NT], AF.Ln, [psn.b], [rn.b], bias=EPS, scale=1.0 / HD)
                    ACT(rn.t[:, 0:NT], rn.t[:, 0:NT], AF.Exp, [rn.b], [rn.b], scale=-0.5)
                    STT(rn.t[:, 0:NT], oT.t[:, hi, 0:NT], c_onw.t[:, l:l + 1], rn.t[:, 0:NT], ALU.mult, ALU.mult,
                        [oT.b, c_onw.b, rn.b], [rn.b])
                    TT("pool", yT.t[:, 8 + h, 0:NT], rn.t[:, 0:NT], szb[h].t[:, 0:NT], ALU.mult, [rn.b, szb[h].b],
                       [yT.bs[8 + h]])

            for cc in range(8 if KSTAGE >= 3 else 0):
                wB = load_w(wsc_in[l, cc], wph[("in", l, cc)])
                wC = load_w(wsc_in[l, 8 + cc], wph[("in", l, 8 + cc)])
                wX = load_w(wsc_in[l, 16 + cc], wph[("in", l, 16 + cc)])
                wZ = load_w(wsc_in[l, 24 + cc], wph[("in", l, 24 + cc)])
                t1 = scr()
                ub = scr()
                acc = scr()
                sz = scr()
                pC = proj(wC, NT)
                ACT(t1.t[:, 0:NT], pC.t[:, 0:NT], AF.Copy, [pC.b], [t1.b])
                pX = proj(wX, NT)
                CP("pool", ub.t[:, 0:2], tailA[l].t[:, cc, :], [tailA[l].b], [ub.b])
                TT("dve", ub.t[:, 2:2 + NT], t1.t[:, 0:NT], pX.t[:, 0:NT], ALU.mult, [t1.b, pX.b], [ub.b])
                CP("pool", tailA[l].t[:, cc, :], ub.t[:, NT:NT + 2], [ub.b], [tailA[l].b])
                w0 = l * 24 + cc * 3
                TS("dve", acc.t[:, 0:NT], ub.t[:, 0:NT], c_caw.t[:, w0:w0 + 1], ALU.mult, [ub.b, c_caw.b], [acc.b])
                for j in (1, 2):
                    STT(acc.t[:, 0:NT], ub.t[:, j:j + NT], c_caw.t[:, w0 + j:w0 + j + 1], acc.t[:, 0:NT],
                        ALU.mult, ALU.add, [ub.b, c_caw.b, acc.b], [acc.b])
                pZ = proj(wZ, NT)
                ACT(sz.t[:, 0:NT], pZ.t[:, 0:NT], AF.Silu, [pZ.b], [sz.b])
                pB = proj(wB, NT)
                TT("dve", acc.t[:, 0:NT], acc.t[:, 0:NT], pB.t[:, 0:NT], ALU.mult, [acc.b, pB.b], [acc.b])
                TT("pool", yT.t[:, cc, 0:NT], acc.t[:, 0:NT], sz.t[:, 0:NT], ALU.mult, [acc.b, sz.b], [yT.bs[cc]])

            for dg in range(DC if KSTAGE >= 8 else 0):
                wo = load_w(wsc_out[l, dg], wph[("out", l, dg)])
                ps = PJ()
                for mc in range(DC):
                    MM(ps.t[:, 0:NT], wo.t[:, mc, :], yT.t[:, mc, 0:NT], [wo.b, yT.bs[mc]], [ps.b],
                       start=(mc == 0), stop=(mc == DC - 1))
                TT("dve", hT.t[:, dg, 0:NT], hT.t[:, dg, 0:NT], ps.t[:, 0:NT], ALU.add, [hT.bs[dg], ps.b], [hT.bs[dg]])

        def load_rows(src_ap, R, off, key):
            DMA("sp", stg.t[0:R, :], src_ap, key, (), [stg.b])
            for dcg in range(4):
                ps = PG()
                for k in range(4):
                    dc = dcg * 4 + k
                    TR(ps.t[:, k * 128:k * 128 + R], stg.t[0:R, dc * 128:(dc + 1) * 128], ident.t[0:R, 0:R],
                       [stg.b, ident.b], [ps.b])
                P.op("act", lambda e, ps=ps, dcg=dcg: e.activation(
                    out=hT.t[:, dcg * 4:dcg * 4 + 4, off:off + R],
                    in_=ps.t[:, :].rearrange("p (a b) -> p a b", a=4)[:, :, 0:R], func=AF.Copy),
                    [ps.b], hT.bs[dcg * 4:dcg * 4 + 4])

        def store_rows(dst_ap, R, off, key):
            for dcg in range(4):
                ps = PG()
                for k in range(4):
                    dc = dcg * 4 + k
                    TR(ps.t[0:R, k * 128:(k + 1) * 128], hT.t[:, dc, off:off + R], ident.t[:], [hT.bs[dc], ident.b], [ps.b])
                CP("act" if dcg % 2 else "dve", stg.t[0:R, dcg * 512:(dcg + 1) * 512], ps.t[0:R, :], [ps.b], [stg.b])
            DMA("sp", dst_ap, stg.t[0:R, :], key, [stg.b], (), final=True)

        def tile(NT, loads, stores):
            if KSTAGE < 1:
                return
            for (src, R, off) in loads:
                load_rows(src, R, off, "xin")
            for l in range(2):
                if KSTAGE >= 2:
                    layer(l, NT)
            if stores:
                rmsnorm_fm(NT, lambda dc: c_fnw.t[:, dc:dc + 1], lambda dc: hT.t[:, dc, 0:NT], hT.bs, c_fnw.b)
                for (dst, R, off) in stores:
                    store_rows(dst, R, off, "yout")

        def state_bufs():
            r = []
            for l in range(2):
                r += S[l].bs + [tailA[l].b, tailQ[l].b]
            return r

        for l in range(2):
            DMA("sp", tailA[l].t[:].rearrange("p a b -> p (a b)"), sca[:, l * 16:(l + 1) * 16], "stinA%d" % l, (), [tailA[l].b])
            DMA("sp", tailQ[l].t[:].rearrange("p a b -> p (a b)"), scq[:, l * 72:(l + 1) * 72], "stinQ%d" % l, (), [tailQ[l].b])
            DMA("sp", S[l].t[:], sdl[l].rearrange("h k v -> k h v"), "stinS%d" % l, (), S[l].bs)
        tile(64, [(xs[0:64, :], 64, 0)], [(ys[0:64, :], 64, 0)])
        for l in range(2):
            DMA("sp", o_sca[:, l * 16:(l + 1) * 16], tailA[l].t[:].rearrange("p a b -> p (a b)"),
                                   "soA%d" % l, [tailA[l].b], (), final=True)
            DMA("sp", o_scq[:, l * 72:(l + 1) * 72], tailQ[l].t[:].rearrange("p a b -> p (a b)"),
                                   "soQ%d" % l, [tailQ[l].b], (), final=True)
            DMA("sp", o_sdl[l].rearrange("h k v -> k h v"), S[l].t[:], "soS%d" % l, S[l].bs, (), final=True)
        for l in range(2):
            MS("pool", tailA[l].t[:], 0.0, [tailA[l].b])
            MS("pool", tailQ[l].t[:], 0.0, [tailQ[l].b])
            MS("pool", S[l].t[:], 0.0, S[l].bs)
        MS("pool", hT.t[:, :, 0:48], 0.0, hT.bs)
        tile(64, [(meta[0:NMETA, :], NMETA, 48)], [])
        for ti in range(n_big):
            t0 = ti * NTB
            tile(NTB, [(xp[t0 + b * 128:t0 + (b + 1) * 128, :], 128, b * 128) for b in range(4)],
                 [(yp[t0 + b * 128:t0 + (b + 1) * 128, :], 128, b * 128) for b in range(4)])
        for l in range(2):
            DMA("sp", o_pca[:, l * 16:(l + 1) * 16], tailA[l].t[:].rearrange("p a b -> p (a b)"),
                                   "soA%d" % l, [tailA[l].b], (), final=True)
            DMA("sp", o_pcq[:, l * 72:(l + 1) * 72], tailQ[l].t[:].rearrange("p a b -> p (a b)"),
                                   "soQ%d" % l, [tailQ[l].b], (), final=True)
            DMA("sp", o_pdl[l].rearrange("h k v -> k h v"), S[l].t[:], "soS%d" % l, S[l].bs, (), final=True)
        P.schedule(reorder=not bool(_os.environ.get("KNOREORDER")))
        P.emit()
    return nc


_CACHE = {}


def _layouts(inp):
    f = lambda a: np.ascontiguousarray(a, dtype=np.float32)
    d = {}
    d["normw"] = f(inp["norm_w"].reshape(2, DC, 128).transpose(2, 0, 1).reshape(128, 2 * DC))
    d["fnw"] = f(inp["final_norm_w"].reshape(DC, 128).T)
    d["caw"] = f(inp["conv_a_w"].reshape(2, 3, 8, 128).transpose(3, 0, 2, 1).reshape(128, 48))
    d["cqw"] = f(inp["conv_qkv_w"].reshape(2, 4, 24, 128).transpose(3, 0, 2, 1).reshape(128, 192))
    d["alog"] = f(inp["a_log"].T)
    d["dtb"] = f(inp["dt_bias"].T)
    d["onw"] = f(inp["o_norm_w"].T)
    d["meta"] = f(inp["meta_tokens"])
    d["w_in"] = f(inp["w_in"])
    d["w_out"] = f(inp["w_out"])
    return d


def run(inp, n_big, n_cores=8):
    key = n_big
    if key not in _CACHE:
        _CACHE[key] = build(n_big)
    nc = _CACHE[key]
    shared = _layouts(inp)
    f = lambda a: np.ascontiguousarray(a, dtype=np.float32)
    seq = n_big * NTB
    nb = min(inp["x_prompt"].shape[0], n_cores)
    owners = [0, 1, 4, 5][:nb] if n_cores == 8 else list(range(nb))
    zx = np.zeros((max(seq, 128), D), np.float32)
    zmeta = np.zeros((NMETA, D), np.float32)
    in_maps = []
    for c in range(n_cores):
        m = dict(shared)
        if c in owners:
            m["xp"] = f(inp["x_prompt"][owners.index(c), :max(seq, 128)])
        else:
            m["xp"] = zx
            m["meta"] = zmeta
        m["xs"] = f(inp["x_sample"][c])
        m["sca"] = f(inp["state_conv_a"][:, c].reshape(2, 2, 8, 128).transpose(3, 0, 2, 1).reshape(128, 32))
        m["scq"] = f(inp["state_conv_qkv"][:, c].reshape(2, 3, 24, 128).transpose(3, 0, 2, 1).reshape(128, 144))
        m["sdl"] = f(inp["state_delta"][:, c])
        in_maps.append(m)
    res = run_bass_kernel_spmd(nc, in_maps, core_ids=list(range(n_cores)))
    R = res.results

    def unA(a):
        return np.asarray(a).reshape(128, 2, 8, 2).transpose(1, 3, 2, 0).reshape(2, 2, 1024)

    def unQ(a):
        return np.asarray(a).reshape(128, 2, 24, 3).transpose(1, 3, 2, 0).reshape(2, 3, 3072)
    y_prompt = np.stack([np.asarray(R[b]["yp"])[:seq] for b in owners]).astype(np.float32)
    y_sample = np.stack([np.asarray(R[c]["ys"]) for c in range(n_cores)]).astype(np.float32)
    p_a = np.stack([unA(R[b]["pca"]) for b in owners], axis=1).astype(np.float32)
    p_q = np.stack([unQ(R[b]["pcq"]) for b in owners], axis=1).astype(np.float32)
    p_s = np.stack([np.asarray(R[b]["pdl"]) for b in owners], axis=1).astype(np.float32)
    s_a = np.stack([unA(R[c]["sca_o"]) for c in range(n_cores)], axis=1).astype(np.float32)
    s_q = np.stack([unQ(R[c]["scq_o"]) for c in range(n_cores)], axis=1).astype(np.float32)
    s_s = np.stack([np.asarray(R[c]["sdl_o"]) for c in range(n_cores)], axis=1).astype(np.float32)
    return (y_prompt, y_sample, p_a, p_q, p_s, s_a, s_q, s_s)


def kernel(**inputs):
    inp = {k: np.asarray(v) for k, v in inputs.items()}
    return run(inp, inp["x_prompt"].shape[1] // NTB)
```

```python
import contextlib
import numpy as np
import concourse.bass as bass
import concourse.mybir as mybir
from concourse.bass_utils import run_bass_kernel_spmd

F32 = mybir.dt.float32
BF16 = mybir.dt.bfloat16
AF = mybir.ActivationFunctionType
ALU = mybir.AluOpType

D = 2048
DC = 16
NH = 8
HD = 128
DP = 8208
NMETA = 16
EPS = 1e-6
NTB = 512
HB = 4
BIG = 1.0e9

import os as _os
SAME_ENGINE_SYNC = not bool(_os.environ.get("KNOSES"))


class Buf:
    __slots__ = ("name", "w", "r", "nowaw")

    def __init__(self, name="", nowaw=False):
        self.name = name
        self.w = None
        self.r = []
        self.nowaw = nowaw


class Prog:
    ENG = ("pe", "act", "dve", "pool", "sp")

    def __init__(self, nc):
        self.nc = nc
        self.q = {e: [] for e in self.ENG}
        self.n = {e: 0 for e in self.ENG}
        self.seen = {e: {} for e in self.ENG}
        self.dma_total = {}
        self.waited = {e: set() for e in self.ENG}
        self.recs = []
        self.tsets = {}
        self.est_time = 0.0

    def _deps(self, e, reads, writes):
        deps = {}

        def add(t):
            if t is None:
                return
            k, v = t
            if deps.get(k, 0) < v:
                deps[k] = v
        for b in reads:
            add(b.w)
        for b in writes:
            if not b.nowaw:
                add(b.w)
            for t in b.r:
                add(t)
        waits = []
        for k, v in deps.items():
            if k == e and (e == "pe" or not SAME_ENGINE_SYNC):
                continue
            if self.seen[e].get(k, 0) >= v:
                continue
            self.seen[e][k] = v
            waits.append((k, v))
            if k in self.waited:
                self.waited[k].add(v)
        return waits

    def _post(self, t, reads, writes):
        for b in reads:
            if len(b.r) > 24:
                d = {}
                for k, v in b.r:
                    if d.get(k, 0) < v:
                        d[k] = v
                b.r = list(d.items())
            b.r.append(t)
        for b in writes:
            b.w = t
            b.r = []

    def op(self, e, fn, reads=(), writes=(), cost=0.3, tset=0):
        self.recs.append((e, fn, tuple(reads), tuple(writes), None, 1, cost, False))
        if tset:
            self.tsets[len(self.recs) - 1] = tset
        return None

    def dma(self, e, fn, semkey, reads=(), writes=(), n=1, cost=4.0, final=False):
        self.recs.append((e, fn, tuple(reads), tuple(writes), semkey, n, cost, final))
        return None

    def schedule(self, reorder=True):
        import heapq
        recs = self.recs
        N = len(recs)
        last_w = {}
        readers = {}
        deps = [None] * N
        for i, (e, fn, reads, writes, semkey, n, cost, final) in enumerate(recs):
            d = set()
            for b in reads:
                w = last_w.get(id(b))
                if w is not None:
                    d.add(w)
            for b in writes:
                w = last_w.get(id(b))
                if w is not None:
                    d.add(w)
                for r in readers.get(id(b), ()):
                    d.add(r)
            d.discard(i)
            deps[i] = d
            for b in reads:
                readers.setdefault(id(b), []).append(i)
            for b in writes:
                last_w[id(b)] = i
                readers[id(b)] = []
        order = list(range(N))
        if reorder:
            succ = [[] for _ in range(N)]
            indeg = [0] * N
            for i in range(N):
                indeg[i] = len(deps[i])
                for d in deps[i]:
                    succ[d].append(i)
            finish = [0.0] * N
            eng_free = {e: 0.0 for e in self.ENG}
            pending = {e: [] for e in self.ENG}
            avail = {e: [] for e in self.ENG}
            for i in range(N):
                if indeg[i] == 0:
                    heapq.heappush(pending[recs[i][0]], (0.0, i))
            order = []
            cur_tset = [0]
            LOOK = 6000
            nxt_unsched = 0
            done = [False] * N
            while len(order) < N:
                best = None
                while nxt_unsched < N and done[nxt_unsched]:
                    nxt_unsched += 1
                for e in self.ENG:
                    pe_, av = pending[e], avail[e]
                    while pe_ and pe_[0][0] <= eng_free[e]:
                        rt, i = heapq.heappop(pe_)
                        heapq.heappush(av, i)
                    cand = None
                    if av and e == "act" and len(av) > 1 and cur_tset[0]:
                        pick = None
                        for ii in heapq.nsmallest(6, av):
                            ts = self.tsets.get(ii, 0)
                            if ts == 0 or ts == cur_tset[0]:
                                pick = ii
                                break
                        if pick is not None and pick != av[0] and pick - av[0] < 400:
                            av.remove(pick)
                            heapq.heapify(av)
                            top = av[0]
                            heapq.heappush(av, pick)
                            cand = (eng_free[e], pick, e, "pick")
                    if cand is not None:
                        pass
                    elif av:
                        cand = (eng_free[e], av[0], e, True)
                    elif pe_:
                        cand = (pe_[0][0], pe_[0][1], e, False)
                    if cand is None:
                        continue
                    if cand[1] > nxt_unsched + LOOK:
                        cand = (cand[0] + 1e6, cand[1], e, cand[3])
                    if best is None or cand[:2] < best[:2]:
                        best = cand
                st, i, e, from_av = best
                if st >= 1e6:
                    st -= 1e6
                if from_av == "pick":
                    avail[e].remove(i)
                    heapq.heapify(avail[e])
                elif from_av:
                    heapq.heappop(avail[e])
                else:
                    heapq.heappop(pending[e])
                if e == "act" and self.tsets.get(i, 0):
                    cur_tset[0] = self.tsets[i]
                rec = recs[i]
                start = max(st, eng_free[e])
                if rec[4] is not None:
                    eng_free[e] = start + 0.06
                    finish[i] = start + rec[6]
                else:
                    eng_free[e] = start + rec[6]
                    finish[i] = eng_free[e]
                done[i] = True
                order.append(i)
                for s_ in succ[i]:
                    indeg[s_] -= 1
                    if indeg[s_] == 0:
                        es = recs[s_][0]
                        rt = 0.0
                        for d in deps[s_]:
                            lat = 0.06 if recs[d][0] == es and recs[d][4] is None else 0.25
                            rt = max(rt, finish[d] + lat)
                        heapq.heappush(pending[es], (rt, s_))
            self.est_time = max(finish) if N else 0.0
        finals = []
        for i in order:
            e, fn, reads, writes, semkey, n, cost, final = recs[i]
            if semkey is None:
                self._op(e, fn, reads, writes)
            else:
                t = self._dma(e, fn, semkey, reads, writes, n)
                if final:
                    finals.append(t)
        fin = {}
        for (k, v) in finals:
            fin[k] = max(fin.get(k, 0), v)
        self.wait_all("sp", list(fin.items()))

    def _op(self, e, fn, reads=(), writes=()):
        waits = self._deps(e, reads, writes)
        self.n[e] += 1
        t = (e, self.n[e])
        self.q[e].append([waits, fn, t, None])
        self._post(t, reads, writes)
        return t

    def _dma(self, e, fn, semkey, reads=(), writes=(), n=1):
        waits = self._deps(e, reads, writes)
        self.dma_total[semkey] = self.dma_total.get(semkey, 0) + 16 * n
        t = (semkey, self.dma_total[semkey])
        self.q[e].append([waits, fn, t, semkey])
        self._post(t, reads, writes)
        return t

    def wait_all(self, e, tickets):
        waits = []
        for (k, v) in tickets:
            if self.seen[e].get(k, 0) >= v:
                continue
            self.seen[e][k] = v
            waits.append((k, v))
            if k in self.waited:
                self.waited[k].add(v)
        self.q[e].append([waits, None, None, None])

    def emit(self):
        nc = self.nc
        handles = {"pe": "tensor", "act": "scalar", "dve": "vector", "pool": "gpsimd", "sp": "sync"}
        with contextlib.ExitStack() as st:
            sems = {}
            for e in self.ENG:
                sems[e] = st.enter_context(nc.semaphore("s_" + e))
            for k in self.dma_total:
                sems[k] = st.enter_context(nc.semaphore("d_" + str(k)))
            vmap = {}
            for e in self.ENG:
                m = {}
                c = 0
                for i in sorted(self.waited[e]):
                    c += 1
                    m[i] = c
                vmap[e] = m
            waited = self.waited
            import os
            if os.environ.get("KDEBUG"):
                print("est_time_us", self.est_time)
                print("instr counts", self.n, "signalled", {e: len(self.waited[e]) for e in self.ENG},
                      "dma", {k: v // 16 for k, v in self.dma_total.items() if v > 64})
            block = st.enter_context(nc.Block())
            for e in self.ENG:
                def body(eng, q=self.q[e], e=e):
                    for waits, fn, t, semkey in q:
                        for (k, v) in waits:
                            if k in vmap:
                                eng.wait_ge(sems[k], vmap[k][v])
                            else:
                                eng.wait_ge(sems[k], v)
                        if fn is None:
                            continue
                        ins = fn(eng)
                        if semkey is not None:
                            if isinstance(ins, (list, tuple)):
                                for i_ in ins:
                                    i_.then_inc(sems[semkey], 16)
                            else:
                                ins.then_inc(sems[semkey], 16)
                        elif t[1] in waited[e]:
                            ins.then_inc(sems[e], 1)
                getattr(block, handles[e])(body)


class TB:
    def __init__(self, t, name, nsub=1):
        self.t = t
        self.b = Buf(name)
        self.bs = [Buf(name + str(i)) for i in range(nsub)] if nsub > 1 else [self.b]


def build(n_big):
    seq = n_big * NTB
    nc = bass.Bass("TRN2", target_bir_lowering=False)

    def din(name, shape, dt=F32):
        return nc.dram_tensor(name, list(shape), dt, kind="ExternalInput").ap()

    def dout(name, shape, dt=F32):
        return nc.dram_tensor(name, list(shape), dt, kind="ExternalOutput").ap()

    xp = din("xp", [max(seq, 128), D])
    xs = din("xs", [64, D])
    sca = din("sca", [128, 2 * 8 * 2])
    scq = din("scq", [128, 2 * 24 * 3])
    sdl = din("sdl", [2, NH, HD, HD])
    meta = din("meta", [NMETA, D])
    normw = din("normw", [128, 2 * DC])
    w_in = din("w_in", [2, D, DP])
    caw = din("caw", [128, 2 * 8 * 3])
    cqw = din("cqw", [128, 2 * 24 * 4])
    alog = din("alog", [8, 2])
    dtb = din("dtb", [8, 2])
    onw = din("onw", [128, 2])
    w_out = din("w_out", [2, D, D])
    fnw = din("fnw", [128, DC])

    yp = dout("yp", [max(seq, 128), D])
    ys = dout("ys", [64, D])
    o_pca = dout("pca", [128, 2 * 8 * 2])
    o_pcq = dout("pcq", [128, 2 * 24 * 3])
    o_pdl = dout("pdl", [2, NH, HD, HD])
    o_sca = dout("sca_o", [128, 2 * 8 * 2])
    o_scq = dout("scq_o", [128, 2 * 24 * 3])
    o_sdl = dout("sdl_o", [2, NH, HD, HD])

    NGI = 64
    wsc_in = nc.dram_tensor("wsc_in", [2, NGI, 128, DC * 128], BF16, kind="Internal").ap()
    wsc_bd = nc.dram_tensor("wsc_bd", [2, 2, 128, DC * 8], BF16, kind="Internal").ap()
    wsc_out = nc.dram_tensor("wsc_out", [2, DC, 128, DC * 128], BF16, kind="Internal").ap()

    KDBG = bool(_os.environ.get("KDBG"))
    if KDBG:
        d_posm = dout("d_posm", [128, 128]); d_negm = dout("d_negm", [128, 128])
        d_L = dout("d_L", [128, 128], BF16); d_Ed = dout("d_Ed", [128, 128]); d_P = dout("d_P", [128, 256], BF16)
        d_sl = dout("d_sl", [128, 128])
    dbg_done = {}
    P = Prog(nc)
    st = contextlib.ExitStack()
    with st:
        def sb(name, shape, dt=F32, nsub=1):
            return TB(st.enter_context(nc.sbuf_tensor(name, list(shape), dt)), name, nsub)

        hT = sb("hT", [128, DC, NTB], F32, DC)
        xn = sb("xn", [128, DC, NTB], BF16, DC)
        yT = sb("yT", [128, DC, NTB], BF16, DC)
        NW = 6
        wt = [sb("wt%d" % i, [128, DC, 128], BF16) for i in range(NW)]
        stg = sb("stg", [128, D], F32)
        oT = TB(stg.t[:, :].rearrange("p (a b) -> p a b", a=HB), "oT")
        oT.b = stg.b
        oT.bs = [stg.b]
        NSC = 12
        SC = [sb("sc%d" % i, [128, NTB + 4], F32) for i in range(NSC)]
        sqb = sb("sqb", [128, NTB], BF16)
        qTb = sb("qTb", [128, NTB], BF16)
        kTb = sb("kTb", [128, NTB], BF16)
        vbf = sb("vbf", [128, NTB], BF16)
        NBLK = NTB // 128
        vb = sb("vb", [128, NBLK, 128], BF16)
        kbg = sb("kbg", [128, NBLK, 128], BF16)
        qgT = [sb("qgT%d" % i, [128, NTB], BF16) for i in range(NH)]
        qkmT = [sb("qkmT%d" % i, [128, NBLK, 128], BF16) for i in range(HB)]
        kdec = [sb("kdec%d" % i, [128, NBLK, 128], BF16) for i in range(HB)]
        wTz = [sb("wTz%d" % i, [128, NBLK, 2, 128], BF16) for i in range(HB)]
        uu = sb("uu", [128, NBLK, HB, 128], BF16)
        szb = [sb("szb%d" % i, [128, NTB], BF16) for i in range(NH)]
        egl = sb("egl", [128, NH, NTB // 64], F32)
        vn = sb("vn", [128, HB, 128], BF16)
        Dm = sb("Dm", [128, NBLK, 128], F32)
        EdB = sb("EdB", [128, NBLK, 128], F32)
        EdTB = sb("EdTB", [128, NBLK, 128], F32)
        T1 = sb("T1", [128, NBLK, 128], BF16)
        Lk = [sb("Lk%d" % i, [128, NBLK, 128], BF16) for i in range(2)]
        PQ = [sb("PQ%d" % i, [128, NBLK, 256], BF16) for i in range(2)]
        ident8 = sb("ident8", [128, NBLK, 128], BF16)
        S = [sb("S%d" % l, [128, NH, 128], F32, NH) for l in range(2)]
        Sbf = sb("Sbf", [128, NH, 128], BF16, NH)
        tailA = [sb("tailA%d" % l, [128, 8, 2], F32) for l in range(2)]
        tailQ = [sb("tailQ%d" % l, [128, 24, 3], F32) for l in range(2)]
        r_beta = sb("r_beta", [8, NTB], F32)
        r_g = sb("r_g", [8, NTB], F32)
        r_gc = sb("r_gc", [8, NTB], F32)
        NB8 = NBLK * 8
        tm_gc = sb("tm_gc", [128, NB8], F32)
        tm_beta = sb("tm_beta", [128, NB8], F32)
        tm_gl = sb("tm_gl", [128, NB8], F32)
        tm_bg = sb("tm_bg", [128, NB8], F32)
        tm_dk = sb("tm_dk", [128, NB8], F32)
        ident = sb("ident", [128, 128], F32)
        identb = sb("identb", [128, 128], BF16)
        onesb = sb("onesb", [128, 128], BF16)
        onesf = sb("onesf", [128, 128], F32)
        negm = sb("negm", [128, 1, 128], F32)
        posm = sb("posm", [128, 1, 128], F32)
        sel = sb("sel", [8, NH, 128], F32)
        sellast = sb("sellast", [128, 128], F32)
        rmask = sb("rmask", [8, NTB], F32)
        c_normw = sb("c_normw", [128, 2 * DC], F32)
        c_fnw = sb("c_fnw", [128, DC], F32)
        c_caw = sb("c_caw", [128, 2 * 8 * 3], F32)
        c_cqw = sb("c_cqw", [128, 2 * 24 * 4], F32)
        c_onw = sb("c_onw", [128, 2], F32)
        c_alog = sb("c_alog", [8, 2], F32)
        c_dtb = sb("c_dtb", [8, 2], F32)
        c_nA = sb("c_nA", [8, 2], F32)
        wbd = [sb("wbd%d" % l, [128, 2, DC, 8], BF16) for l in range(2)]

        NPJ = 4
        pj = [TB(st.enter_context(nc.psum_tensor("pj%d" % i, [128, 512], F32)), "pj%d" % i) for i in range(NPJ)]
        NPG = 4
        pg = [TB(st.enter_context(nc.psum_tensor("pg%d" % i, [128, 512], F32)), "pg%d" % i) for i in range(NPG)]
        ctr = {"pj": 0, "pg": 0, "w": 0, "sc": 0, "ch": 0}

        def PJ():
            ctr["pj"] += 1
            return pj[ctr["pj"] % NPJ]

        def PG():
            ctr["pg"] += 1
            return pg[ctr["pg"] % NPG]

        def _fs(ap):
            n = 1
            for d_ in ap.shape[1:]:
                n *= int(d_)
            return n

        def _ecost(eng, n):
            if eng == "dve":
                return 0.08 + n / 960.0
            if eng == "act":
                return 0.22 + n / 1100.0
            return 0.25 + n / 480.0

        def MM(out, lhsT, rhs, r, w, start=True, stop=True):
            c = 0.035 + (4 if rhs.dtype == F32 else 1) * _fs(rhs) / 1900.0
            return P.op("pe", lambda e: e.matmul(out, lhsT=lhsT, rhs=rhs, start=start, stop=stop), r, w, cost=c)

        def TR(out, in_, idn, r, w):
            c = 0.06 + (4 if in_.dtype == F32 else 1) * _fs(idn) / 1900.0
            return P.op("pe", lambda e: e.transpose(out, in_, idn), r, w, cost=c)

        def ACT(out, in_, func, r, w, bias=None, scale=None):
            kw = {}
            if bias is not None:
                kw["bias"] = bias
            if scale is not None:
                kw["scale"] = scale
            ts = {AF.Silu: 1, AF.Exp: 2, AF.Ln: 2, AF.Sigmoid: 3}.get(func, 0)
            return P.op("act", lambda e: e.activation(out=out, in_=in_, func=func, **kw), r, w,
                        cost=_ecost("act", _fs(out)), tset=ts)

        def TS(eng, out, in0, s1, op0, r, w, s2=None, op1=None):
            c = _ecost(eng, _fs(out))
            if op1 is None:
                return P.op(eng, lambda e: e.tensor_scalar(out=out, in0=in0, scalar1=s1, scalar2=None, op0=op0), r, w, cost=c)
            return P.op(eng, lambda e: e.tensor_scalar(out=out, in0=in0, scalar1=s1, scalar2=s2, op0=op0, op1=op1), r, w, cost=c)

        def STT(out, in0, scalar, in1, op0, op1, r, w):
            return P.op("dve", lambda e: e.scalar_tensor_tensor(out=out, in0=in0, scalar=scalar, in1=in1, op0=op0, op1=op1),
                        r, w, cost=_ecost("dve", _fs(out)))

        def TT(eng, out, in0, in1, op, r, w):
            return P.op(eng, lambda e: e.tensor_tensor(out=out, in0=in0, in1=in1, op=op), r, w, cost=_ecost(eng, _fs(out)))

        def CP(eng, out, in_, r, w):
            c = _ecost(eng, _fs(out))
            if eng == "act":
                return P.op(eng, lambda e: e.activation(out=out, in_=in_, func=AF.Copy), r, w, cost=c)
            return P.op(eng, lambda e: e.tensor_copy(out=out, in_=in_), r, w, cost=c)

        def RECIP(out, in_, r, w):
            return P.op("dve", lambda e: e.reciprocal(out=out, in_=in_), r, w, cost=0.1 + _fs(out) / 170.0)

        def MS(eng, ap, val, w):
            return P.op(eng, lambda e: e.memset(ap, val), (), w, cost=0.1 + _fs(ap) / 1000.0)

        def AFS(out, in_, pattern, cmp, fill, base, cm, r, w):
            return P.op("pool", lambda e: e.affine_select(out=out, in_=in_, pattern=pattern, compare_op=cmp,
                                                         fill=fill, base=base, channel_multiplier=cm), r, w, cost=0.4)

        def DMA(eng, out, in_, key, r, w, slow=False, final=False):
            nbytes = 1
            for d_ in out.shape:
                nbytes *= int(d_)
            nbytes *= (4 if out.dtype == F32 else 2)
            c = 2.0 + nbytes / 150e3
            if slow:
                return P.dma(eng, lambda e: e.dma_start(out=out, in_=in_, allow_slow_non_contiguous=True), key, r, w,
                             cost=c, final=final)
            return P.dma(eng, lambda e: e.dma_start(out=out, in_=in_), key, r, w, cost=c, final=final)

        cb = Buf("consts")
        for i, (dst, src) in enumerate([(c_normw, normw), (c_fnw, fnw), (c_caw, caw), (c_cqw, cqw), (c_onw, onw),
                                        (c_alog, alog), (c_dtb, dtb)]):
            DMA("sp", dst.t[:], src, "cst%d" % i, (), [dst.b])
        MS("pool", onesf.t[:], 1.0, [onesf.b])
        MS("pool", onesb.t[:], 1.0, [onesb.b])
        AFS(ident.t[:], onesf.t[:, 0:128], [[-1, 128]], ALU.is_equal, 0.0, 0, 1, [onesf.b], [ident.b])
        CP("pool", identb.t[:], ident.t[:], [ident.b], [identb.b])
        MS("pool", negm.t[:], 0.0, [negm.b])
        MS("pool", posm.t[:], 0.0, [posm.b])
        for cblk in range(2):
            cs = slice(cblk * 64, (cblk + 1) * 64)
            AFS(negm.t[:, 0, cs], negm.t[:, 0, cs], [[1, 64]], ALU.is_ge, -BIG, cblk * 64, -1, [negm.b], [negm.b])
            AFS(negm.t[:, 0, cs], negm.t[:, 0, cs], [[0, 64]], ALU.is_ge, -BIG, -cblk * 64, 1, [negm.b], [negm.b])
            AFS(posm.t[:, 0, cs], posm.t[:, 0, cs], [[-1, 64]], ALU.is_ge, BIG, -cblk * 64 - 1, 1, [posm.b], [posm.b])
            AFS(posm.t[:, 0, cs], posm.t[:, 0, cs], [[0, 64]], ALU.is_ge, BIG, (cblk + 1) * 64 - 1, -1, [posm.b], [posm.b])
        for blk in range(NBLK):
            CP("pool", ident8.t[:, blk, :], ident.t[:, :], [ident.b], [ident8.b])
        for h in range(NH):
            AFS(sel.t[:, h, :], onesf.t[0:8, 0:128], [[0, 128]], ALU.is_equal, 0.0, -h, 1, [onesf.b], [sel.b])
        MS("pool", sellast.t[:], 1.0, [sellast.b])
        for cblk in range(2):
            cs = slice(cblk * 64, (cblk + 1) * 64)
            AFS(sellast.t[:, cs], sellast.t[:, cs], [[0, 64]], ALU.is_equal, 0.0, -(cblk * 64 + 63), 1,
                [sellast.b], [sellast.b])
        MS("pool", rmask.t[:], 1.0, [rmask.b])
        for c in range(NTB // 64):
            MS("pool", rmask.t[:, c * 64:c * 64 + 1], 0.0, [rmask.b])
        for i in range(HB):
            MS("pool", wTz[i].t[:], 0.0, [wTz[i].b])
        ACT(c_nA.t[:], c_alog.t[:], AF.Exp, [c_alog.b], [c_nA.b])
        TS("dve", c_nA.t[:], c_nA.t[:], -1.0, ALU.mult, [c_nA.b], [c_nA.b])

        wph = {}
        for l in range(2):
            src = w_in[l].rearrange("(c p) n -> p c n", p=128)
            srco = w_out[l].rearrange("(c p) n -> p c n", p=128)
            phases = []
            phases.append([("bd", j) for j in range(2)] + [("in", k * 8 + 0) for k in range(4)])
            phases.append([("in", k * 8 + cc) for cc in range(1, 4) for k in range(4)])
            phases.append([("in", k * 8 + cc) for cc in range(4, 8) for k in range(4)])
            for h0 in range(0, 8, 2):
                phases.append([("in", 32 + k * 8 + h) for h in (h0, h0 + 1) for k in range(4)])
            phases.append([("out", g) for g in range(0, 8)])
            phases.append([("out", g) for g in range(8, 16)])
            for pi, ph in enumerate(phases):
                key = "wc%d_%d" % (l, pi)
                pb = Buf(key, nowaw=True)
                for (kind, g) in ph:
                    if kind == "in":
                        DMA("pool", wsc_in[l, g].rearrange("p (c n) -> p c n", c=DC), src[:, :, g * 128:(g + 1) * 128],
                            key, (), [pb])
                        wph[("in", l, g)] = pb
                    elif kind == "bd":
                        DMA("pool", wsc_bd[l, g].rearrange("p (c n) -> p c n", c=DC),
                            src[:, :, 8192 + g * 8:8192 + (g + 1) * 8], key, (), [pb], slow=True)
                        wph[("bd", l)] = pb
                    else:
                        DMA("pool", wsc_out[l, g].rearrange("p (c n) -> p c n", c=DC),
                            srco[:, :, g * 128:(g + 1) * 128], key, (), [pb])
                        wph[("out", l, g)] = pb
        for l in range(2):
            for j in range(2):
                DMA("sp", wbd[l].t[:, j, :, :], wsc_bd[l, j].rearrange("p (c n) -> p c n", c=DC), "cstb%d%d" % (l, j),
                    [wph[("bd", l)]], [wbd[l].b])

        def load_w(src_ap, pb):
            ctr["w"] += 1
            i = ctr["w"] % NW
            DMA("sp", wt[i].t[:], src_ap.rearrange("p (c n) -> p c n", c=DC), "w%d" % i, [pb], [wt[i].b])
            return wt[i]

        def proj(wtile, NT, M=128, lhs=None):
            ps = PJ()
            for dc in range(DC):
                l_ap = wtile.t[:, dc, 0:M] if lhs is None else lhs(dc)
                MM(ps.t[0:M, 0:NT], l_ap, xn.t[:, dc, 0:NT], [wtile.b, xn.bs[dc]], [ps.b],
                   start=(dc == 0), stop=(dc == DC - 1))
            return ps

        def scr():
            ctr["sc"] += 1
            return SC[ctr["sc"] % NSC]

        def rmsnorm_fm(NT, wcol, out_fn, out_bufs, wb):
            rstd = scr()
            ps = PG()
            for dc in range(DC):
                ACT(yT.t[:, dc, 0:NT], hT.t[:, dc, 0:NT], AF.Square, [hT.bs[dc]], [yT.bs[dc]])
            for dc in range(DC):
                MM(ps.t[:, 0:NT], onesb.t[:], yT.t[:, dc, 0:NT], [onesb.b, yT.bs[dc]], [ps.b],
                   start=(dc == 0), stop=(dc == DC - 1))
            ACT(rstd.t[:, 0:NT], ps.t[:, 0:NT], AF.Ln, [ps.b], [rstd.b], bias=EPS, scale=1.0 / D)
            ACT(rstd.t[:, 0:NT], rstd.t[:, 0:NT], AF.Exp, [rstd.b], [rstd.b], scale=-0.5)
            for dc in range(DC):
                STT(out_fn(dc), hT.t[:, dc, 0:NT], wcol(dc), rstd.t[:, 0:NT], ALU.mult, ALU.mult,
                    [hT.bs[dc], rstd.b, wb], [out_bufs[dc]])


        def v3(ap2, n):
            return ap2.rearrange("p (b i) -> p b i", i=n)

        def intra(l, NT, h, hi, gcb):
            BS = min(128, NT)
            nblk = NT // BS
            n8 = nblk * 8
            W = nblk * BS

            def tmb(tm, n):
                return tm.t[0:BS, 0:n8].rearrange("p (b e) -> p b e", e=8)[:, :, h:h + 1].broadcast_to([BS, nblk, n])

            def blkv(ap2, n):
                return ap2.rearrange("p (b i) -> p b i", i=n)
            TT("dve", Dm.t[0:BS, 0:nblk, 0:BS], blkv(gcb.t[0:BS, 0:NT], BS), tmb(tm_gc, BS), ALU.subtract,
               [gcb.b, tm_gc.b], [Dm.b])
            TT("dve", EdTB.t[0:BS, 0:nblk, 0:BS], Dm.t[0:BS, 0:nblk, 0:BS],
               negm.t[0:BS, 0:1, 0:BS].broadcast_to([BS, nblk, BS]), ALU.min, [Dm.b, negm.b], [EdTB.b])
            TT("dve", EdB.t[0:BS, 0:nblk, 0:BS], Dm.t[0:BS, 0:nblk, 0:BS],
               posm.t[0:BS, 0:1, 0:BS].broadcast_to([BS, nblk, BS]), ALU.max, [Dm.b, posm.b], [EdB.b])
            ACT(EdTB.t[0:BS, 0:nblk, 0:BS], EdTB.t[0:BS, 0:nblk, 0:BS], AF.Exp, [EdTB.b], [EdTB.b])
            ACT(EdB.t[0:BS, 0:nblk, 0:BS], EdB.t[0:BS, 0:nblk, 0:BS], AF.Exp, [EdB.b], [EdB.b], scale=-1.0)
            pk = PG()
            pk_b = pk.t[:].bitcast(BF16)
            for blk in range(nblk):
                TR(pk_b[0:BS, blk * 128:(blk + 1) * 128], kTb.t[:, blk * BS:(blk + 1) * BS], identb.t[:],
                   [kTb.b, identb.b], [pk.b])
            pv = PG()
            pv_b = pv.t[:].bitcast(BF16)
            for blk in range(nblk):
                TR(pv_b[0:BS, blk * 128:(blk + 1) * 128], vbf.t[:, blk * BS:(blk + 1) * BS], identb.t[:],
                   [vbf.b, identb.b], [pv.b])
            TT("dve", kbg.t[0:BS, 0:nblk, :], blkv(pk_b[0:BS, 0:nblk * 128], 128), tmb(tm_bg, 128), ALU.mult,
               [pk.b, tm_bg.b], [kbg.b])
            TT("dve", kdec[hi].t[0:BS, 0:nblk, :], blkv(pk_b[0:BS, 0:nblk * 128], 128), tmb(tm_dk, 128), ALU.mult,
               [pk.b, tm_dk.b], [kdec[hi].b])
            TT("dve", vb.t[0:BS, 0:nblk, :], blkv(pv_b[0:BS, 0:nblk * 128], 128), tmb(tm_beta, 128), ALU.mult,
               [pv.b, tm_beta.b], [vb.b])
            pkk = PG()
            for blk in range(nblk):
                bsl = slice(blk * BS, (blk + 1) * BS)
                MM(pkk.t[0:BS, bsl], kTb.t[:, bsl], kTb.t[:, bsl], [kTb.b], [pkk.b])
            pqk = PG()
            for blk in range(nblk):
                bsl = slice(blk * BS, (blk + 1) * BS)
                MM(pqk.t[0:BS, bsl], kTb.t[:, bsl], qTb.t[:, bsl], [kTb.b, qTb.b], [pqk.b])
            TT("dve", T1.t[0:BS, 0:nblk, 0:BS], blkv(pkk.t[0:BS, 0:W], BS), tmb(tm_beta, BS), ALU.mult,
               [pkk.b, tm_beta.b], [T1.b])
            cur = 0
            TT("pool", Lk[cur].t[0:BS, 0:nblk, 0:BS], T1.t[0:BS, 0:nblk, 0:BS], EdB.t[0:BS, 0:nblk, 0:BS], ALU.mult,
               [T1.b, EdB.b], [Lk[cur].b])
            TT("dve", qkmT[hi].t[0:BS, 0:nblk, 0:BS], blkv(pqk.t[0:BS, 0:W], BS), EdTB.t[0:BS, 0:nblk, 0:BS], ALU.mult,
               [pqk.b, EdTB.b], [qkmT[hi].b])
            plt = PG()
            plt_b = plt.t[:].bitcast(BF16)
            for blk in range(nblk):
                TR(plt_b[0:BS, blk * BS:(blk + 1) * BS], Lk[cur].t[0:BS, blk, 0:BS], identb.t[0:BS, 0:BS],
                   [Lk[cur].b, identb.b], [plt.b])
            CP("dve", PQ[cur].t[0:BS, 0:nblk, BS:2 * BS], blkv(plt_b[0:BS, 0:W], BS), [plt.b], [PQ[cur].b])
            TT("dve", PQ[cur].t[0:BS, 0:nblk, 0:BS], ident8.t[0:BS, 0:nblk, 0:BS], blkv(plt_b[0:BS, 0:W], BS),
               ALU.subtract, [ident8.b, plt.b], [PQ[cur].b])
            bpb = max(1, 512 // (2 * BS))
            for s_ in range(6):
                first = (s_ == 0)
                last = (s_ == 5)
                nxt = 1 - cur
                cL, cPQ, nPQ = Lk[cur], PQ[cur], PQ[nxt]
                if first:
                    p1 = PG()
                    for blk in range(nblk):
                        MM(p1.t[0:BS, blk * BS:(blk + 1) * BS], cL.t[0:BS, blk, 0:BS], cPQ.t[0:BS, blk, BS:2 * BS],
                           [cL.b, cPQ.b], [p1.b])
                    CP("dve", nPQ.t[0:BS, 0:nblk, BS:2 * BS], blkv(p1.t[0:BS, 0:W], BS), [p1.b], [nPQ.b])
                    CP("dve", nPQ.t[0:BS, 0:nblk, 0:BS], cPQ.t[0:BS, 0:nblk, 0:BS], [cPQ.b], [nPQ.b])
                elif not last:
                    for b0 in range(0, nblk, bpb):
                        nb_ = min(bpb, nblk - b0)
                        p01 = PG()
                        for blk in range(b0, b0 + nb_):
                            o_ = (blk - b0) * 2 * BS
                            MM(p01.t[0:BS, o_:o_ + 2 * BS], cL.t[0:BS, blk, 0:BS], cPQ.t[0:BS, blk, 0:2 * BS],
                               [cL.b, cPQ.b], [p01.b])
                        pv3 = blkv(p01.t[0:BS, 0:nb_ * 2 * BS], 2 * BS)
                        TT("dve", nPQ.t[0:BS, b0:b0 + nb_, 0:BS], pv3[:, :, 0:BS], cPQ.t[0:BS, b0:b0 + nb_, 0:BS], ALU.add,
                           [p01.b, cPQ.b], [nPQ.b])
                        CP("dve", nPQ.t[0:BS, b0:b0 + nb_, BS:2 * BS], pv3[:, :, BS:2 * BS], [p01.b], [nPQ.b])
                else:
                    p0 = PG()
                    for blk in range(nblk):
                        MM(p0.t[0:BS, blk * BS:(blk + 1) * BS], cL.t[0:BS, blk, 0:BS], cPQ.t[0:BS, blk, 0:BS],
                           [cL.b, cPQ.b], [p0.b])
                    TT("dve", nPQ.t[0:BS, 0:nblk, 0:BS], blkv(p0.t[0:BS, 0:W], BS), cPQ.t[0:BS, 0:nblk, 0:BS], ALU.add,
                       [p0.b, cPQ.b], [nPQ.b])
                if not last:
                    p2 = PG()
                    for blk in range(nblk):
                        MM(p2.t[0:BS, blk * BS:(blk + 1) * BS], cPQ.t[0:BS, blk, BS:2 * BS], cL.t[0:BS, blk, 0:BS],
                           [cL.b, cPQ.b], [p2.b])
                    CP("act", Lk[nxt].t[0:BS, 0:nblk, 0:BS], blkv(p2.t[0:BS, 0:W], BS), [p2.b], [Lk[nxt].b])
                cur = nxt
            Tt = PQ[cur]
            pw = PG()
            for blk in range(nblk):
                MM(pw.t[:, blk * BS:(blk + 1) * BS], kbg.t[0:BS, blk, :], Tt.t[0:BS, blk, 0:BS], [kbg.b, Tt.b], [pw.b])
            pw3 = blkv(pw.t[:, 0:W], BS)
            CP("act", wTz[hi].t[:, 0:nblk, 0, 0:64], pw3[:, :, 0:64], [pw.b], [wTz[hi].b])
            if BS == 128:
                CP("act", wTz[hi].t[:, 0:nblk, 1, 64:128], pw3[:, :, 64:128], [pw.b], [wTz[hi].b])
            pu = PG()
            for blk in range(nblk):
                MM(pu.t[0:BS, blk * 128:(blk + 1) * 128], Tt.t[0:BS, blk, 0:BS], vb.t[0:BS, blk, :],
                   [Tt.b, vb.b], [pu.b])
            CP("dve", uu.t[0:BS, 0:nblk, hi, :], blkv(pu.t[0:BS, 0:nblk * 128], 128), [pu.b], [uu.b])

        import os
        KSTAGE = int(os.environ.get("KSTAGE", "99"))
        KSUB = int(os.environ.get("KSUB", "99"))

        out_tickets = []
        def layer(l, NT):
            BS = min(128, NT)
            nblk = NT // BS
            nch = NT // 64
            rmsnorm_fm(NT, lambda dc: c_normw.t[:, l * DC + dc:l * DC + dc + 1],
                       lambda dc: xn.t[:, dc, 0:NT], xn.bs, c_normw.b)

            psb = proj(wbd[l], NT, M=8, lhs=lambda dc: wbd[l].t[:, 0, dc, :])
            ACT(r_beta.t[:, 0:NT], psb.t[0:8, 0:NT], AF.Sigmoid, [psb.b], [r_beta.b])
            psa = proj(wbd[l], NT, M=8, lhs=lambda dc: wbd[l].t[:, 1, dc, :])
            ACT(r_g.t[:, 0:NT], psa.t[0:8, 0:NT], AF.Exp, [psa.b, c_dtb.b], [r_g.b], bias=c_dtb.t[:, l:l + 1])
            ACT(r_g.t[:, 0:NT], r_g.t[:, 0:NT], AF.Ln, [r_g.b], [r_g.b], bias=1.0)
            TS("dve", r_g.t[:, 0:NT], r_g.t[:, 0:NT], c_nA.t[:, l:l + 1], ALU.mult, [r_g.b, c_nA.b], [r_g.b])
            P.op("dve", lambda e: e.tensor_tensor_scan(out=r_gc.t[:, 0:NT], data0=rmask.t[:, 0:NT],
                                                       data1=r_g.t[:, 0:NT], initial=0.0,
                                                       op0=ALU.mult, op1=ALU.add),
                 [rmask.b, r_g.b], [r_gc.b])
            pst = PG()
            for blk in range(nblk):
                TR(pst.t[0:BS, blk * 8:blk * 8 + 8], r_gc.t[:, blk * BS:(blk + 1) * BS], ident.t[0:8, 0:8],
                   [r_gc.b, ident.b], [pst.b])
                TR(pst.t[0:BS, 128 + blk * 8:128 + blk * 8 + 8], r_beta.t[:, blk * BS:(blk + 1) * BS],
                   ident.t[0:8, 0:8], [r_beta.b, ident.b], [pst.b])
            n8 = nblk * 8
            CP("dve", tm_gc.t[0:BS, 0:n8], pst.t[0:BS, 0:n8], [pst.b], [tm_gc.b])
            CP("dve", tm_beta.t[0:BS, 0:n8], pst.t[0:BS, 128:128 + n8], [pst.b], [tm_beta.b])
            psl = PG()
            MM(psl.t[0:BS, 0:n8], sellast.t[0:BS, 0:BS], tm_gc.t[0:BS, 0:n8], [sellast.b, tm_gc.b], [psl.b])
            TT("dve", tm_gl.t[0:BS, 0:n8], psl.t[0:BS, 0:n8], tm_gc.t[0:BS, 0:n8], ALU.subtract,
               [psl.b, tm_gc.b], [tm_gl.b])
            ACT(tm_dk.t[0:BS, 0:n8], tm_gl.t[0:BS, 0:n8], AF.Exp, [tm_gl.b], [tm_dk.b])
            ACT(tm_bg.t[0:BS, 0:n8], tm_gc.t[0:BS, 0:n8], AF.Exp, [tm_gc.b], [tm_bg.b])
            TT("dve", tm_bg.t[0:BS, 0:n8], tm_bg.t[0:BS, 0:n8], tm_beta.t[0:BS, 0:n8], ALU.mult,
               [tm_bg.b, tm_beta.b], [tm_bg.b])

            for hb in range(NH // HB if KSTAGE >= 4 else 0):
                for hi in range(HB):
                    h = hb * HB + hi
                    wq = load_w(wsc_in[l, 32 + h], wph[("in", l, 32 + h)])
                    wk = load_w(wsc_in[l, 40 + h], wph[("in", l, 40 + h)])
                    wv = load_w(wsc_in[l, 48 + h], wph[("in", l, 48 + h)])
                    wz = load_w(wsc_in[l, 56 + h], wph[("in", l, 56 + h)])
                    cs = []
                    for i3, wti in enumerate((wq, wk, wv)):
                        ps = proj(wti, NT)
                        ch = i3 * 8 + h
                        pre = scr()
                        acc = scr()
                        CP("pool", pre.t[:, 0:3], tailQ[l].t[:, ch, :], [tailQ[l].b], [pre.b])
                        ACT(pre.t[:, 3:3 + NT], ps.t[:, 0:NT], AF.Copy, [ps.b], [pre.b])
                        CP("pool", tailQ[l].t[:, ch, :], pre.t[:, NT:NT + 3], [pre.b], [tailQ[l].b])
                        w0 = l * 96 + ch * 4
                        TS("dve", acc.t[:, 0:NT], pre.t[:, 0:NT], c_cqw.t[:, w0:w0 + 1], ALU.mult,
                           [pre.b, c_cqw.b], [acc.b])
                        for j in (1, 2, 3):
                            STT(acc.t[:, 0:NT], pre.t[:, j:j + NT], c_cqw.t[:, w0 + j:w0 + j + 1], acc.t[:, 0:NT],
                                ALU.mult, ALU.add, [pre.b, c_cqw.b, acc.b], [acc.b])
                        ACT(acc.t[:, 0:NT], acc.t[:, 0:NT], AF.Silu, [acc.b], [acc.b])
                        cs.append(acc)
                    qc, kc, vc = cs
                    psz = proj(wz, NT)
                    ACT(szb[h].t[:, 0:NT], psz.t[:, 0:NT], AF.Silu, [psz.b], [szb[h].b])
                    for (src, dstT, scale) in ((qc, qTb, HD ** -0.5), (kc, kTb, 1.0)):
                        ACT(sqb.t[:, 0:NT], src.t[:, 0:NT], AF.Square, [src.b], [sqb.b])
                        psn = PG()
                        MM(psn.t[:, 0:NT], onesb.t[:], sqb.t[:, 0:NT], [onesb.b, sqb.b], [psn.b])
                        rn = scr()
                        ACT(rn.t[:, 0:NT], psn.t[:, 0:NT], AF.Ln, [psn.b], [rn.b], bias=EPS, scale=1.0)
                        ACT(rn.t[:, 0:NT], rn.t[:, 0:NT], AF.Exp, [rn.b], [rn.b], scale=-0.5)
                        STT(dstT.t[:, 0:NT], src.t[:, 0:NT], scale, rn.t[:, 0:NT], ALU.mult, ALU.mult,
                            [src.b, rn.b], [dstT.b])
                    CP("pool", vbf.t[:, 0:NT], vc.t[:, 0:NT], [vc.b], [vbf.b])
                    psg = PG()
                    MM(psg.t[:, 0:NT], sel.t[:, h, :], r_gc.t[:, 0:NT], [sel.b, r_gc.b], [psg.b])
                    gcb = scr()
                    egcb = scr()
                    ACT(gcb.t[:, 0:NT], psg.t[:, 0:NT], AF.Copy, [psg.b], [gcb.b])
                    ACT(egcb.t[:, 0:NT], psg.t[:, 0:NT], AF.Exp, [psg.b], [egcb.b])
                    CP("pool", egl.t[:, h, 0:nch], egcb.t[:, 0:NT].rearrange("p (c t) -> p c t", t=64)[:, :, 63],
                       [egcb.b], [egl.b])
                    TT("dve", qgT[h].t[:, 0:NT], qTb.t[:, 0:NT], egcb.t[:, 0:NT], ALU.mult, [qTb.b, egcb.b],
                       [qgT[h].b])
                    if KSTAGE >= 5:
                        intra(l, NT, h, hi, gcb)

                hs = [hb * HB + hi for hi in range(HB)]
                for hi, h in enumerate(hs):
                    CP("act", Sbf.t[:, h, :], S[l].t[:, h, :], [S[l].bs[h]], [Sbf.bs[h]])
                for c in range(nch if KSTAGE >= 6 else 0):
                    blk = (c * 64) // BS
                    par = c % (BS // 64)
                    rows = slice(par * 64, par * 64 + 64)
                    csl = slice(c * 64, c * 64 + 64)
                    pw_ = PG()
                    for hi, h in enumerate(hs):
                        MM(pw_.t[0:BS, hi * 128:(hi + 1) * 128], wTz[hi].t[:, blk, par, 0:BS], Sbf.t[:, h, :],
                           [wTz[hi].b, Sbf.bs[h]], [pw_.b])
                    TT("dve", vn.t[rows, :, :], uu.t[rows, blk, :, :],
                       pw_.t[rows, :].rearrange("p (a b) -> p a b", a=HB), ALU.subtract,
                       [uu.b, pw_.b], [vn.b])
                    po = PG()
                    pS = PG()
                    for hi, h in enumerate(hs):
                        MM(po.t[:, hi * 64:(hi + 1) * 64], Sbf.t[:, h, :], qgT[h].t[:, csl],
                           [Sbf.bs[h], qgT[h].b], [po.b], start=True, stop=False)
                        MM(po.t[:, hi * 64:(hi + 1) * 64], vn.t[rows, hi, :], qkmT[hi].t[rows, blk, par * 64:par * 64 + 64],
                           [vn.b, qkmT[hi].b], [po.b], start=False, stop=True)
                        MM(pS.t[:, hi * 128:(hi + 1) * 128], kdec[hi].t[rows, blk, :], vn.t[rows, hi, :],
                           [kdec[hi].b, vn.b], [pS.b])
                    CP("act", oT.t[:, :, csl], po.t[:, 0:HB * 64].rearrange("p (a b) -> p a b", a=HB), [po.b], [oT.b])
                    for hi, h in enumerate(hs):
                        STT(S[l].t[:, h, :], S[l].t[:, h, :], egl.t[:, h, c:c + 1], pS.t[:, hi * 128:(hi + 1) * 128],
                            ALU.mult, ALU.add, [S[l].bs[h], egl.b, pS.b], [S[l].bs[h]])
                        CP("act", Sbf.t[:, h, :], S[l].t[:, h, :], [S[l].bs[h]], [Sbf.bs[h]])

                for hi, h in enumerate(hs if KSTAGE >= 7 else []):
                    ACT(sqb.t[:, 0:NT], oT.t[:, hi, 0:NT], AF.Square, [oT.b], [sqb.b])
                    psn = PG()
                    MM(psn.t[:, 0:NT], onesb.t[:], sqb.t[:, 0:NT], [onesb.b, sqb.b], [psn.b])
                    rn = scr()
                    ACT(rn.t[:, 0:NT], psn.t[:, 0:NT], AF.Ln, [psn.b], [rn.b], bias=EPS, scale=1.0 / HD)
                    ACT(rn.t[:, 0:NT], rn.t[:, 0:NT], AF.Exp, [rn.b], [rn.b], scale=-0.5)
                    STT(rn.t[:, 0:NT], oT.t[:, hi, 0:NT], c_onw.t[:, l:l + 1], rn.t[:, 0:NT], ALU.mult, ALU.mult,
                        [oT.b, c_onw.b, rn.b], [rn.b])
                    TT("pool", yT.t[:, 8 + h, 0:NT], rn.t[:, 0:NT], szb[h].t[:, 0:NT], ALU.mult, [rn.b, szb[h].b],
                       [yT.bs[8 + h]])

            for cc in range(8 if KSTAGE >= 3 else 0):
                wB = load_w(wsc_in[l, cc], wph[("in", l, cc)])
                wC = load_w(wsc_in[l, 8 + cc], wph[("in", l, 8 + cc)])
                wX = load_w(wsc_in[l, 16 + cc], wph[("in", l, 16 + cc)])
                wZ = load_w(wsc_in[l, 24 + cc], wph[("in", l, 24 + cc)])
                pC = proj(wC, NT)
                pX = proj(wX, NT)
                pB = proj(wB, NT)
                pZ = proj(wZ, NT)
                t1 = scr()
                ub = scr()
                acc = scr()
                sz = scr()
                ACT(t1.t[:, 0:NT], pC.t[:, 0:NT], AF.Copy, [pC.b], [t1.b])
                CP("pool", ub.t[:, 0:2], tailA[l].t[:, cc, :], [tailA[l].b], [ub.b])
                TT("dve", ub.t[:, 2:2 + NT], t1.t[:, 0:NT], pX.t[:, 0:NT], ALU.mult, [t1.b, pX.b], [ub.b])
                CP("pool", tailA[l].t[:, cc, :], ub.t[:, NT:NT + 2], [ub.b], [tailA[l].b])
                w0 = l * 24 + cc * 3
                TS("dve", acc.t[:, 0:NT], ub.t[:, 0:NT], c_caw.t[:, w0:w0 + 1], ALU.mult, [ub.b, c_caw.b], [acc.b])
                for j in (1, 2):
                    STT(acc.t[:, 0:NT], ub.t[:, j:j + NT], c_caw.t[:, w0 + j:w0 + j + 1], acc.t[:, 0:NT],
                        ALU.mult, ALU.add, [ub.b, c_caw.b, acc.b], [acc.b])
                ACT(sz.t[:, 0:NT], pZ.t[:, 0:NT], AF.Silu, [pZ.b], [sz.b])
                TT("dve", acc.t[:, 0:NT], acc.t[:, 0:NT], pB.t[:, 0:NT], ALU.mult, [acc.b, pB.b], [acc.b])
                TT("pool", yT.t[:, cc, 0:NT], acc.t[:, 0:NT], sz.t[:, 0:NT], ALU.mult, [acc.b, sz.b], [yT.bs[cc]])

            for dg in range(DC if KSTAGE >= 8 else 0):
                wo = load_w(wsc_out[l, dg], wph[("out", l, dg)])
                ps = PJ()
                for mc in range(DC):
                    MM(ps.t[:, 0:NT], wo.t[:, mc, :], yT.t[:, mc, 0:NT], [wo.b, yT.bs[mc]], [ps.b],
                       start=(mc == 0), stop=(mc == DC - 1))
                TT("dve", hT.t[:, dg, 0:NT], hT.t[:, dg, 0:NT], ps.t[:, 0:NT], ALU.add, [hT.bs[dg], ps.b], [hT.bs[dg]])

        def load_rows(src_ap, R, off, key):
            DMA("sp", stg.t[0:R, :], src_ap, key, (), [stg.b])
            for dcg in range(4):
                ps = PG()
                for k in range(4):
                    dc = dcg * 4 + k
                    TR(ps.t[:, k * 128:k * 128 + R], stg.t[0:R, dc * 128:(dc + 1) * 128], ident.t[0:R, 0:R],
                       [stg.b, ident.b], [ps.b])
                P.op("act", lambda e, ps=ps, dcg=dcg: e.activation(
                    out=hT.t[:, dcg * 4:dcg * 4 + 4, off:off + R],
                    in_=ps.t[:, :].rearrange("p (a b) -> p a b", a=4)[:, :, 0:R], func=AF.Copy),
                    [ps.b], hT.bs[dcg * 4:dcg * 4 + 4])

        def store_rows(dst_ap, R, off, key):
            for dcg in range(4):
                ps = PG()
                for k in range(4):
                    dc = dcg * 4 + k
                    TR(ps.t[0:R, k * 128:(k + 1) * 128], hT.t[:, dc, off:off + R], ident.t[:], [hT.bs[dc], ident.b], [ps.b])
                CP("act" if dcg % 2 else "dve", stg.t[0:R, dcg * 512:(dcg + 1) * 512], ps.t[0:R, :], [ps.b], [stg.b])
            DMA("sp", dst_ap, stg.t[0:R, :], key, [stg.b], (), final=True)

        def tile(NT, loads, stores):
            if KSTAGE < 1:
                return
            for (src, R, off) in loads:
                load_rows(src, R, off, "xin")
            for l in range(2):
                if KSTAGE >= 2:
                    layer(l, NT)
            if stores:
                rmsnorm_fm(NT, lambda dc: c_fnw.t[:, dc:dc + 1], lambda dc: hT.t[:, dc, 0:NT], hT.bs, c_fnw.b)
                for (dst, R, off) in stores:
                    store_rows(dst, R, off, "yout")

        def state_bufs():
            r = []
            for l in range(2):
                r += S[l].bs + [tailA[l].b, tailQ[l].b]
            return r

        for l in range(2):
            DMA("sp", tailA[l].t[:].rearrange("p a b -> p (a b)"), sca[:, l * 16:(l + 1) * 16], "stinA%d" % l, (), [tailA[l].b])
            DMA("sp", tailQ[l].t[:].rearrange("p a b -> p (a b)"), scq[:, l * 72:(l + 1) * 72], "stinQ%d" % l, (), [tailQ[l].b])
            DMA("sp", S[l].t[:], sdl[l].rearrange("h k v -> k h v"), "stinS%d" % l, (), S[l].bs)
        tile(64, [(xs[0:64, :], 64, 0)], [(ys[0:64, :], 64, 0)])
        for l in range(2):
            DMA("sp", o_sca[:, l * 16:(l + 1) * 16], tailA[l].t[:].rearrange("p a b -> p (a b)"),
                                   "soA%d" % l, [tailA[l].b], (), final=True)
            DMA("sp", o_scq[:, l * 72:(l + 1) * 72], tailQ[l].t[:].rearrange("p a b -> p (a b)"),
                                   "soQ%d" % l, [tailQ[l].b], (), final=True)
            DMA("sp", o_sdl[l].rearrange("h k v -> k h v"), S[l].t[:], "soS%d" % l, S[l].bs, (), final=True)
        for l in range(2):
            MS("pool", tailA[l].t[:], 0.0, [tailA[l].b])
            MS("pool", tailQ[l].t[:], 0.0, [tailQ[l].b])
            MS("pool", S[l].t[:], 0.0, S[l].bs)
        MS("pool", hT.t[:, :, 0:48], 0.0, hT.bs)
        tile(64, [(meta[0:NMETA, :], NMETA, 48)], [])
        for ti in range(n_big):
            t0 = ti * NTB
            tile(NTB, [(xp[t0 + b * 128:t0 + (b + 1) * 128, :], 128, b * 128) for b in range(4)],
                 [(yp[t0 + b * 128:t0 + (b + 1) * 128, :], 128, b * 128) for b in range(4)])
        for l in range(2):
            DMA("sp", o_pca[:, l * 16:(l + 1) * 16], tailA[l].t[:].rearrange("p a b -> p (a b)"),
                                   "soA%d" % l, [tailA[l].b], (), final=True)
            DMA("sp", o_pcq[:, l * 72:(l + 1) * 72], tailQ[l].t[:].rearrange("p a b -> p (a b)"),
                                   "soQ%d" % l, [tailQ[l].b], (), final=True)
            DMA("sp", o_pdl[l].rearrange("h k v -> k h v"), S[l].t[:], "soS%d" % l, S[l].bs, (), final=True)
        P.schedule(reorder=not bool(_os.environ.get("KNOREORDER")))
        P.emit()
    return nc


_CACHE = {}


def _layouts(inp):
    f = lambda a: np.ascontiguousarray(a, dtype=np.float32)
    d = {}
    d["normw"] = f(inp["norm_w"].reshape(2, DC, 128).transpose(2, 0, 1).reshape(128, 2 * DC))
    d["fnw"] = f(inp["final_norm_w"].reshape(DC, 128).T)
    d["caw"] = f(inp["conv_a_w"].reshape(2, 3, 8, 128).transpose(3, 0, 2, 1).reshape(128, 48))
    d["cqw"] = f(inp["conv_qkv_w"].reshape(2, 4, 24, 128).transpose(3, 0, 2, 1).reshape(128, 192))
    d["alog"] = f(inp["a_log"].T)
    d["dtb"] = f(inp["dt_bias"].T)
    d["onw"] = f(inp["o_norm_w"].T)
    d["meta"] = f(inp["meta_tokens"])
    d["w_in"] = f(inp["w_in"])
    d["w_out"] = f(inp["w_out"])
    return d


def run(inp, n_big, n_cores=8):
    key = n_big
    if key not in _CACHE:
        _CACHE[key] = build(n_big)
    nc = _CACHE[key]
    shared = _layouts(inp)
    f = lambda a: np.ascontiguousarray(a, dtype=np.float32)
    seq = n_big * NTB
    nb = min(inp["x_prompt"].shape[0], n_cores)
    owners = [0, 1, 4, 5][:nb] if n_cores == 8 else list(range(nb))
    zx = np.zeros((max(seq, 128), D), np.float32)
    zmeta = np.zeros((NMETA, D), np.float32)
    in_maps = []
    for c in range(n_cores):
        m = dict(shared)
        if c in owners:
            m["xp"] = f(inp["x_prompt"][owners.index(c), :max(seq, 128)])
        else:
            m["xp"] = zx
            m["meta"] = zmeta
        m["xs"] = f(inp["x_sample"][c])
        m["sca"] = f(inp["state_conv_a"][:, c].reshape(2, 2, 8, 128).transpose(3, 0, 2, 1).reshape(128, 32))
        m["scq"] = f(inp["state_conv_qkv"][:, c].reshape(2, 3, 24, 128).transpose(3, 0, 2, 1).reshape(128, 144))
        m["sdl"] = f(inp["state_delta"][:, c])
        in_maps.append(m)
    res = run_bass_kernel_spmd(nc, in_maps, core_ids=list(range(n_cores)))
    R = res.results

    def unA(a):
        return np.asarray(a).reshape(128, 2, 8, 2).transpose(1, 3, 2, 0).reshape(2, 2, 1024)

    def unQ(a):
        return np.asarray(a).reshape(128, 2, 24, 3).transpose(1, 3, 2, 0).reshape(2, 3, 3072)
    y_prompt = np.stack([np.asarray(R[b]["yp"])[:seq] for b in owners]).astype(np.float32)
    y_sample = np.stack([np.asarray(R[c]["ys"]) for c in range(n_cores)]).astype(np.float32)
    p_a = np.stack([unA(R[b]["pca"]) for b in owners], axis=1).astype(np.float32)
    p_q = np.stack([unQ(R[b]["pcq"]) for b in owners], axis=1).astype(np.float32)
    p_s = np.stack([np.asarray(R[b]["pdl"]) for b in owners], axis=1).astype(np.float32)
    s_a = np.stack([unA(R[c]["sca_o"]) for c in range(n_cores)], axis=1).astype(np.float32)
    s_q = np.stack([unQ(R[c]["scq_o"]) for c in range(n_cores)], axis=1).astype(np.float32)
    s_s = np.stack([np.asarray(R[c]["sdl_o"]) for c in range(n_cores)], axis=1).astype(np.float32)
    return (y_prompt, y_sample, p_a, p_q, p_s, s_a, s_q, s_s)


def kernel(**inputs):
    inp = {k: np.asarray(v) for k, v in inputs.items()}
    return run(inp, inp["x_prompt"].shape[1] // NTB)
```
